# Optimizing a Trainium2 kernel written in Bass

```python
import math
import jax, jax.numpy as jnp
from jax import lax
import numpy as np

D_MODEL = 1024
BATCH = 8
SEQ = 2048
DEPTH = 4

N_MIXERS = 3
HEAD_DIM = 64
MIX_WIDTH = 768
N_MIX_HEADS = MIX_WIDTH // HEAD_DIM
MEM_LEN = 256
N_MEM_HEADS = 4
MEM_WIDTH = N_MEM_HEADS * HEAD_DIM
MIXING_WIDTH = MIX_WIDTH + MEM_WIDTH
Q_BLOCK = 128
POOL_WINDOWS = (2, 4, 8, 16)
N_POOL_GROUPS = len(POOL_WINDOWS)
POOL_GROUP_DIM = MIX_WIDTH // N_POOL_GROUPS
Q_LORA_RANK = 384
KV_LORA_RANK = 256
QK_NOPE_DIM = 64
QK_ROPE_DIM = 32
V_HEAD_DIM = 64
ROPE_THETA = 10000.0
D_FF = 2816
ALPHA = (2 * DEPTH) ** 0.25
BETA = (8 * DEPTH) ** -0.25
LN_EPS = 1e-5
RMS_EPS = 1e-6
N_FOX = (DEPTH + 2) // 3
N_POOL = (DEPTH + 1) // 3
N_MLA = DEPTH // 3
FOX_IN = 3 * MIX_WIDTH + N_MIX_HEADS + MEM_WIDTH
POOL_IN = MIX_WIDTH + MEM_WIDTH
MLA_IN = Q_LORA_RANK + KV_LORA_RANK + QK_ROPE_DIM + MEM_WIDTH

kernel_name = "hybrid_fox_pool_mla_macaron_deepnorm"

F32 = jnp.float32


def layer_norm(x, g, b):
    xf = x.astype(F32)
    mu = jnp.mean(xf, axis=-1, keepdims=True)
    var = jnp.mean(jnp.square(xf - mu), axis=-1, keepdims=True)
    return ((xf - mu) * lax.rsqrt(var + LN_EPS) * g.astype(F32) + b.astype(F32)).astype(x.dtype)


def rms_norm(x, g):
    xf = x.astype(F32)
    ms = jnp.mean(jnp.square(xf), axis=-1, keepdims=True)
    return (xf * lax.rsqrt(ms + RMS_EPS) * g.astype(F32)).astype(x.dtype)


def swiglu(x, w_gate, w_up, w_down):
    return (jax.nn.silu(x @ w_gate) * (x @ w_up)) @ w_down


def split_heads(t, n_heads):
    b, s, _ = t.shape
    return t.reshape(b, s, n_heads, -1).transpose(0, 2, 1, 3)


def merge_heads(t):
    b, h, s, d = t.shape
    return t.transpose(0, 2, 1, 3).reshape(b, s, h * d)


def rope(x, cos, sin):
    x1, x2 = jnp.split(x, 2, axis=-1)
    return jnp.concatenate([x1 * cos - x2 * sin, x1 * sin + x2 * cos], axis=-1)


def causal_block_attention(q, k, v, decay=None):
    b, h, s_len, dk = q.shape
    dv = v.shape[-1]
    nb = s_len // Q_BLOCK
    scale = dk ** -0.5
    k_pos = jnp.arange(s_len)
    q_blocks = q.reshape(b, h, nb, Q_BLOCK, dk).transpose(2, 0, 1, 3, 4)
    if decay is None:
        xs = (jnp.arange(nb), q_blocks)
    else:
        xs = (jnp.arange(nb), q_blocks, decay.reshape(b, h, nb, Q_BLOCK).transpose(2, 0, 1, 3))

    def body(args):
        i, q_i = args[0], args[1]
        s = jnp.einsum('bhqd,bhkd->bhqk', q_i, k).astype(F32) * scale
        if decay is not None:
            s = s + args[2][..., :, None] - decay[..., None, :]
        q_pos = i * Q_BLOCK + jnp.arange(Q_BLOCK)
        s = jnp.where(k_pos[None, :] <= q_pos[:, None], s, -jnp.inf)
        p = jax.nn.softmax(s, axis=-1)
        return jnp.einsum('bhqk,bhkd->bhqd', p.astype(v.dtype), v)

    out = lax.map(body, xs)
    return out.transpose(1, 2, 0, 3, 4).reshape(b, h, s_len, dv)


def memory_attention(q, mem, w_kv):
    k, v = jnp.split(mem @ w_kv, 2, axis=-1)
    qh, kh, vh = (split_heads(t, N_MEM_HEADS) for t in (q, k, v))
    s = jnp.einsum('bhqd,bhkd->bhqk', qh, kh).astype(F32) * (HEAD_DIM ** -0.5)
    p = jax.nn.softmax(s, axis=-1)
    return merge_heads(jnp.einsum('bhqk,bhkd->bhqd', p.astype(vh.dtype), vh))


def fox_mixer(h, b_f):
    q, k, v, f_logit = jnp.split(h, [MIX_WIDTH, 2 * MIX_WIDTH, 3 * MIX_WIDTH], axis=-1)
    log_f = jax.nn.log_sigmoid((f_logit + b_f).astype(F32))
    decay = jnp.cumsum(log_f, axis=1).transpose(0, 2, 1)
    o = causal_block_attention(split_heads(q, N_MIX_HEADS), split_heads(k, N_MIX_HEADS),
                               split_heads(v, N_MIX_HEADS), decay)
    return merge_heads(o)


def pool_mixer(h, w_grp, scale):
    b, s_len, _ = h.shape
    u = h.reshape(b, s_len, N_POOL_GROUPS, POOL_GROUP_DIM)
    csum = jnp.cumsum(u.astype(F32), axis=1)
    t = jnp.arange(s_len)
    means = []
    for g, w in enumerate(POOL_WINDOWS):
        c = csum[:, :, g]
        c_prev = jnp.pad(c, ((0, 0), (w, 0), (0, 0)))[:, :s_len]
        count = jnp.minimum(t + 1, w).astype(F32)
        means.append((c - c_prev) / count[None, :, None])
    pooled = jnp.stack(means, axis=2).astype(h.dtype) - u
    y = jnp.einsum('bsgc,gcd->bsgd', pooled, w_grp)
    return y.reshape(b, s_len, MIX_WIDTH) * scale


def mla_mixer(h, cos, sin, q_norm, kv_norm, w_uq, w_ukv):
    b, s_len, _ = h.shape
    c_q, c_kv, k_rope = jnp.split(h, [Q_LORA_RANK, Q_LORA_RANK + KV_LORA_RANK], axis=-1)
    q = split_heads(rms_norm(c_q, q_norm) @ w_uq, N_MIX_HEADS)
    kv = split_heads(rms_norm(c_kv, kv_norm) @ w_ukv, N_MIX_HEADS)
    q_nope, q_pe = jnp.split(q, [QK_NOPE_DIM], axis=-1)
    k_nope, v = jnp.split(kv, [QK_NOPE_DIM], axis=-1)
    q_pe = rope(q_pe, cos[:, None], sin[:, None])
    k_pe = jnp.broadcast_to(rope(k_rope, cos, sin)[:, None],
                            (b, N_MIX_HEADS, s_len, QK_ROPE_DIM))
    q = jnp.concatenate([q_nope, q_pe], axis=-1)
    k = jnp.concatenate([k_nope, k_pe], axis=-1)
    return merge_heads(causal_block_attention(q, k, v))


def setup_inputs(seed: int = 0) -> dict:
    key = jax.random.key(seed)
    ks = jax.random.split(key, 20)

    def nrm(k, shape, std):
        return std * jax.random.normal(k, shape, F32)

    x = jax.random.normal(ks[0], (BATCH, SEQ, D_MODEL), F32)
    mem = jax.random.normal(ks[1], (BATCH, MEM_LEN, D_MODEL), F32)
    start = jax.random.randint(ks[2], (BATCH, 1), 0, 4096, dtype=jnp.int32)
    positions = (start + jnp.arange(SEQ, dtype=jnp.int32)[None, :]).astype(jnp.int32)
    ln_g = 1.0 + nrm(ks[3], (DEPTH, 3, D_MODEL), 0.02)
    ln_b = nrm(ks[4], (DEPTH, 3, D_MODEL), 0.02)
    ffn_w_gate = nrm(ks[5], (DEPTH, 2, D_MODEL, D_FF), D_MODEL ** -0.5)
    ffn_w_up = nrm(ks[6], (DEPTH, 2, D_MODEL, D_FF), D_MODEL ** -0.5)
    ffn_w_down = nrm(ks[7], (DEPTH, 2, D_FF, D_MODEL), BETA * D_FF ** -0.5)
    mem_w_kv = nrm(ks[8], (DEPTH, D_MODEL, 2 * MEM_WIDTH), D_MODEL ** -0.5)
    w_o = nrm(ks[9], (DEPTH, MIXING_WIDTH, D_MODEL), BETA * MIXING_WIDTH ** -0.5)
    fox_w_in = nrm(ks[10], (N_FOX, D_MODEL, FOX_IN), D_MODEL ** -0.5)
    fox_b_f = jax.random.uniform(ks[11], (N_FOX, N_MIX_HEADS), F32, 1.0, 4.0)
    pool_w_in = nrm(ks[12], (N_POOL, D_MODEL, POOL_IN), D_MODEL ** -0.5)
    pool_w_grp = nrm(ks[13], (N_POOL, N_POOL_GROUPS, POOL_GROUP_DIM, POOL_GROUP_DIM),
                     POOL_GROUP_DIM ** -0.5)
    pool_scale = 1.0 + nrm(ks[14], (N_POOL, MIX_WIDTH), 0.02)
    mla_w_in = nrm(ks[15], (N_MLA, D_MODEL, MLA_IN), D_MODEL ** -0.5)
    mla_q_norm = 1.0 + nrm(ks[16], (N_MLA, Q_LORA_RANK), 0.02)
    mla_kv_norm = 1.0 + nrm(ks[17], (N_MLA, KV_LORA_RANK), 0.02)
    mla_w_uq = nrm(ks[18], (N_MLA, Q_LORA_RANK, N_MIX_HEADS * (QK_NOPE_DIM + QK_ROPE_DIM)),
                   Q_LORA_RANK ** -0.5)
    mla_w_ukv = nrm(ks[19], (N_MLA, KV_LORA_RANK, N_MIX_HEADS * (QK_NOPE_DIM + V_HEAD_DIM)),
                    KV_LORA_RANK ** -0.5)
    return {"x": x, "mem": mem, "positions": positions, "ln_g": ln_g, "ln_b": ln_b,
            "ffn_w_gate": ffn_w_gate, "ffn_w_up": ffn_w_up, "ffn_w_down": ffn_w_down,
            "mem_w_kv": mem_w_kv, "w_o": w_o, "fox_w_in": fox_w_in, "fox_b_f": fox_b_f,
            "pool_w_in": pool_w_in, "pool_w_grp": pool_w_grp, "pool_scale": pool_scale,
            "mla_w_in": mla_w_in, "mla_q_norm": mla_q_norm, "mla_kv_norm": mla_kv_norm,
            "mla_w_uq": mla_w_uq, "mla_w_ukv": mla_w_ukv}


def reference(x, mem, positions, ln_g, ln_b, ffn_w_gate, ffn_w_up, ffn_w_down, mem_w_kv,
              w_o, fox_w_in, fox_b_f, pool_w_in, pool_w_grp, pool_scale, mla_w_in,
              mla_q_norm, mla_kv_norm, mla_w_uq, mla_w_ukv):
    inv_freq = ROPE_THETA ** (-jnp.arange(0, QK_ROPE_DIM, 2, dtype=F32) / QK_ROPE_DIM)
    ang = positions.astype(F32)[..., None] * inv_freq
    cos = jnp.cos(ang).astype(x.dtype)
    sin = jnp.sin(ang).astype(x.dtype)

    for i in range(DEPTH):
        kind, j = i % N_MIXERS, i // N_MIXERS
        ff = swiglu(x, ffn_w_gate[i, 0], ffn_w_up[i, 0], ffn_w_down[i, 0])
        x = layer_norm(ALPHA * x + 0.5 * ff, ln_g[i, 0], ln_b[i, 0])
        if kind == 0:
            h = x @ fox_w_in[j]
            main = fox_mixer(h[..., :-MEM_WIDTH], fox_b_f[j])
        elif kind == 1:
            h = x @ pool_w_in[j]
            main = pool_mixer(h[..., :-MEM_WIDTH], pool_w_grp[j], pool_scale[j])
        else:
            h = x @ mla_w_in[j]
            main = mla_mixer(h[..., :-MEM_WIDTH], cos, sin, mla_q_norm[j], mla_kv_norm[j],
                             mla_w_uq[j], mla_w_ukv[j])
        mem_out = memory_attention(h[..., -MEM_WIDTH:], mem, mem_w_kv[i])
        mix = jnp.concatenate([main, mem_out], axis=-1) @ w_o[i]
        x = layer_norm(ALPHA * x + mix, ln_g[i, 1], ln_b[i, 1])
        ff = swiglu(x, ffn_w_gate[i, 1], ffn_w_up[i, 1], ffn_w_down[i, 1])
        x = layer_norm(ALPHA * x + 0.5 * ff, ln_g[i, 2], ln_b[i, 2])
    return x
```

```python
import math
from contextlib import ExitStack
import numpy as np
import concourse.bass as bass
import concourse.mybir as mybir
from concourse.bass_utils import run_bass_kernel_spmd

F32 = mybir.dt.float32
BF16 = mybir.dt.bfloat16
I32 = mybir.dt.int32
AF = mybir.ActivationFunctionType
ALU = mybir.AluOpType

S = 2048
D = 1024
DFF = 2816
NFC = 22
DEPTH = 4
ALPHA = (2 * DEPTH) ** 0.25
LN_EPS = 1e-5
RMS_EPS = 1e-6
ENGS = ["pe", "act", "dve", "pool", "sp"]
NDMA_SEM = {"sp": 16, "pool": 6}


class Sched:
    def __init__(self):
        self.prog = {e: [] for e in ENGS}
        self.count = {}
        self.last_w = {}
        self.readers = {}
        self.seen = {e: {} for e in ENGS}
        self.pe_pr = set()
        self.pe_pw = set()
        self.dma_rr = {"sp": 0, "pool": 0}
        self.pending_bar = {e: None for e in ENGS}
        self.fam_iv = {}
        self.fam_state = {}

    def reg(self, fam, lo, n):
        self.fam_iv[fam] = (lo, lo + n)

    def _fam_update(self, reads, writes, tok):
        for k in set(reads) | set(writes):
            fam = k if isinstance(k, str) else k[0]
            iv = self.fam_iv.get(fam)
            if iv is None:
                continue
            st = self.fam_state.setdefault((fam, iv[0], iv[1]), {"w": {}, "r": {}})
            d = st["w"] if k in writes else st["r"]
            if d.get(tok[0], 0) < tok[1]:
                d[tok[0]] = tok[1]

    def _deps(self, reads, writes):
        deps = {}

        def add(tok):
            if tok is None:
                return
            s, v = tok
            if deps.get(s, 0) < v:
                deps[s] = v
        for k in reads:
            add(self.last_w.get(k))
        for k in writes:
            add(self.last_w.get(k))
            for s, v in self.readers.get(k, {}).items():
                add((s, v))
        wset = set(writes)
        for k in set(reads) | wset:
            fam = k if isinstance(k, str) else k[0]
            iv = self.fam_iv.get(fam)
            if iv is None:
                continue
            for (f2, lo2, hi2), st in self.fam_state.items():
                if f2 == fam and lo2 == iv[0] and hi2 == iv[1]:
                    continue
                if lo2 < iv[1] and iv[0] < hi2:
                    for s, v in st["w"].items():
                        add((s, v))
                    if k in wset:
                        for s, v in st["r"].items():
                            add((s, v))
        return deps

    def emit(self, eng, fn, reads=(), writes=(), signal=True, dma=False):
        deps = self._deps(reads, writes)
        bar = self.pending_bar[eng]
        if bar is not None:
            for s, v in bar.items():
                if deps.get(s, 0) < v:
                    deps[s] = v
            self.pending_bar[eng] = None
        waits = []
        for s, v in deps.items():
            if s == "pe" and eng == "pe":
                continue
            if self.seen[eng].get(s, 0) >= v:
                continue
            self.seen[eng][s] = v
            waits.append((s, v))
        inc = None
        tok = None
        if dma:
            i = self.dma_rr[eng]
            self.dma_rr[eng] = (i + 1) % NDMA_SEM[eng]
            sem = "d%s%d" % (eng, i)
            prev = self.count.get(sem, 0)
            if prev > 0 and self.seen[eng].get(sem, 0) < prev:
                self.seen[eng][sem] = prev
                waits.append((sem, prev))
            self.count[sem] = prev + 16
            tok = (sem, self.count[sem])
            inc = (sem, 16)
        elif signal:
            self.count[eng] = self.count.get(eng, 0) + 1
            tok = (eng, self.count[eng])
            inc = (eng, 1)
        self.prog[eng].append((waits, fn, inc))
        if eng == "pe" and not signal:
            self.pe_pr.update(reads)
            self.pe_pw.update(writes)
            return None
        if eng == "pe":
            reads = set(reads) | self.pe_pr
            writes = set(writes) | self.pe_pw
            self.pe_pr = set()
            self.pe_pw = set()
        self._fam_update(reads, writes, tok)
        for k in writes:
            self.last_w[k] = tok
            self.readers[k] = {}
        for k in reads:
            if k in writes:
                continue
            r = self.readers.setdefault(k, {})
            if r.get(tok[0], 0) < tok[1]:
                r[tok[0]] = tok[1]
        return tok

    def barrier(self):
        snap = dict(self.count)
        for e in ENGS:
            self.pending_bar[e] = dict(snap)


def build_program(n_layers=DEPTH):
    nc = bass.Bass("TRN2", target_bir_lowering=False)
    dr = {}

    def din(name, shape, dt=F32):
        dr[name] = nc.dram_tensor(name, list(shape), dt, kind="ExternalInput").ap()
        return dr[name]
    x_d = din("x", [S, D])
    memT_d = din("memT", [D, 256])
    pos_d = din("positions", [1, S], I32)
    ln_g = din("ln_g", [4, 3, D])
    ln_b = din("ln_b", [4, 3, D])
    wg_d = din("ffn_w_gate", [4, 2, D, DFF])
    wu_d = din("ffn_w_up", [4, 2, D, DFF])
    wd_d = din("ffn_w_down", [4, 2, DFF, D])
    wkv_d = din("mem_w_kv", [4, D, 512])
    wo_d = din("w_o", [4, D, D])
    fox_w = din("fox_w_in", [2, D, 2572])
    fox_b = din("fox_b_f", [2, 12])
    pool_w = din("pool_w_in", [1, D, 1024])
    pool_grp = din("pool_w_grp", [1, 4, 192, 192])
    pool_scale = din("pool_scale", [1, 768])
    mla_w = din("mla_w_in", [1, D, 928])
    mla_qn = din("mla_q_norm", [1, 384])
    mla_kvn = din("mla_kv_norm", [1, 256])
    mla_uq = din("mla_w_uq", [1, 384, 1152])
    mla_ukv = din("mla_w_ukv", [1, 256, 1536])
    out_d = nc.dram_tensor("out", [S, D], F32, kind="ExternalOutput").ap()

    sc = Sched()
    es = ExitStack()
    with es:
        def sb(name, shape, dt):
            return es.enter_context(nc.sbuf_tensor(name, list(shape), dt))
        xres = sb("xres", [128, 16, D], F32)
        ident = sb("ident", [128, 128], F32)
        tri = sb("tri", [128, 128], BF16)
        memT = sb("memT_sb", [128, 8, 256], BF16)
        gb = sb("gb", [128, 2, D], F32)
        epst = sb("epst", [128, 2], F32)
        st6 = sb("st6", [128, 2, 2, 6], F32)
        mv = sb("mv", [128, 2, 2], F32)
        rstd = sb("rstd", [128, 2, 4], F32)
        smallc = sb("smallc", [128, 64], F32)
        wgrp_sb = sb("wgrp_sb", [128, 2, 192], BF16)
        onesb = sb("onesb", [128, 128], BF16)
        identb16 = sb("identb16", [128, 128], BF16)
        rden = sb("rden", [128, 2, 4], F32)
        ARENA = 65536
        arena = sb("arena", [128, ARENA], BF16)
        psum = [es.enter_context(nc.psum_tensor("ps%d" % i, [128, 512], F32)) for i in range(8)]
        sems = {}
        for e in ENGS:
            sems[e] = es.enter_context(nc.semaphore("s_" + e))
        for q in ("sp", "pool"):
            for i in range(NDMA_SEM[q]):
                n = "d%s%d" % (q, i)
                sems[n] = es.enter_context(nc.semaphore(n))

        def carve(off, n_elems_bf16, dt, pattern=None, **kw):
            v = arena[:, off:off + n_elems_bf16]
            if dt == F32:
                v = v.bitcast(F32)
            if pattern:
                v = v.rearrange(pattern, **kw)
            return v

        def PK(b):
            return ("ps", b)

        def dma(q, out, in_, reads, writes):
            return sc.emit(q, lambda e: e.dma_start(out=out, in_=in_), reads, writes, dma=True)

        def mm(out, lhsT, rhs, start, stop, reads, writes, signal, sgc=False):
            if sgc:
                return sc.emit("pe", lambda e: e.matmul(out, lhsT, rhs, start=start, stop=stop, skip_group_check=True),
                               reads, writes, signal=signal)
            return sc.emit("pe", lambda e: e.matmul(out, lhsT, rhs, start=start, stop=stop),
                           reads, writes, signal=signal)

        def tr16(out, in_, reads, writes, signal):
            return sc.emit("pe", lambda e: e.transpose(out, in_, identb16[:]), reads, writes, signal=signal)

        def tr(out, in_, reads, writes, signal):
            return sc.emit("pe", lambda e: e.transpose(out, in_, ident[:]), reads, writes, signal=signal)

        def act(out, in_, func, reads, writes, bias=None, scale=None):
            kw = {}
            if bias is not None:
                kw["bias"] = bias
            if scale is not None:
                kw["scale"] = scale
            return sc.emit("act", lambda e: e.activation(out=out, in_=in_, func=func, **kw), reads, writes)

        def vop(fn, reads, writes, eng="dve"):
            return sc.emit(eng, fn, reads, writes)

        vop(lambda e: e.iota(ident[:], pattern=[[1, 128]], base=0, channel_multiplier=-1,
                             allow_small_or_imprecise_dtypes=True), [], ["ident"], eng="pool")
        vop(lambda e: e.tensor_single_scalar(out=tri[:], in_=ident[:], scalar=0.0, op=ALU.is_ge), ["ident"], ["tri"])
        vop(lambda e: e.tensor_single_scalar(out=ident[:], in_=ident[:], scalar=0.0, op=ALU.is_equal), ["ident", "tri"], ["ident"])
        vop(lambda e: e.memset(epst[:, 0:1], LN_EPS), [], ["eps"])
        vop(lambda e: e.memset(onesb[:, :], 1.0), [], ["onesb"])
        vop(lambda e: e.tensor_copy(out=identb16[:, :], in_=ident[:, :]), ["ident"], ["identb16"])
        vop(lambda e: e.iota(smallc[:, 16:32], pattern=[[1, 16]], base=1, channel_multiplier=0,
                             allow_small_or_imprecise_dtypes=True), [], ["rinv"], eng="pool")
        vop(lambda e: e.reciprocal(out=smallc[:, 16:32], in_=smallc[:, 16:32]), ["rinv"], ["rinv"])
        vop(lambda e: e.memset(epst[:, 1:2], RMS_EPS), ["eps"], ["eps"])
        dma("pool", memT[:], memT_d.rearrange("(c p) m -> p c m", p=128), [], ["memT"])
        for q4 in range(4):
            dma("sp", xres[:, 4 * q4:4 * q4 + 4, :],
                x_d[q4 * 512:(q4 + 1) * 512, :].rearrange("(i p) d -> p i d", p=128),
                [], [("x", 4 * q4 + k) for k in range(4)])

        def ln_tile(i, par):
            xk = ("x", i)
            for h in range(2):
                vop(lambda e, h=h: e.bn_stats(out=st6[:, par, h, :], in_=xres[:, i, h * 512:(h + 1) * 512]),
                    [xk], [("st6", par, h)])
            vop(lambda e: e.bn_aggr(out=mv[:, par, :], in_=st6[:, par].rearrange("p a b -> p (a b)")),
                [("st6", par, 0), ("st6", par, 1)], [("mv", par)])
            act(rstd[:, par, 0:1], mv[:, par, 1:2], AF.Sqrt, [("mv", par), "eps"], [("rs0", par)], bias=epst[:, 0:1])
            vop(lambda e: e.reciprocal(out=rstd[:, par, 1:2], in_=rstd[:, par, 0:1]), [("rs0", par)], [("rs1", par)])
            vop(lambda e: e.tensor_scalar(out=rstd[:, par, 2:3], in0=mv[:, par, 0:1], scalar1=rstd[:, par, 1:2],
                                          scalar2=-1.0, op0=ALU.mult, op1=ALU.mult),
                [("mv", par), ("rs1", par)], [("nmr", par)])
            act(xres[:, i, :], xres[:, i, :], AF.Identity, [xk, ("rs1", par), ("nmr", par)], [xk],
                bias=rstd[:, par, 2:3], scale=rstd[:, par, 1:2])
            vop(lambda e: e.tensor_tensor(out=xres[:, i, :], in0=xres[:, i, :], in1=gb[:, 0, :], op=ALU.mult),
                [xk, "gb"], [xk])
            vop(lambda e: e.tensor_tensor(out=xres[:, i, :], in0=xres[:, i, :], in1=gb[:, 1, :], op=ALU.add),
                [xk, "gb"], [xk])

        def load_gb(l, j):
            dma("sp", gb[:, 0, :], ln_g[l, j, :].partition_broadcast(128), [], ["gb"])
            dma("sp", gb[:, 1, :], ln_b[l, j, :].partition_broadcast(128), [], ["gb"])

        def resid_add(i, h, bank):
            vop(lambda e: e.scalar_tensor_tensor(out=xres[:, i, h * 512:(h + 1) * 512],
                                                 in0=xres[:, i, h * 512:(h + 1) * 512], scalar=float(ALPHA),
                                                 in1=psum[bank][:, :], op0=ALU.mult, op1=ALU.add),
                [("x", i), PK(bank)], [("x", i)])

        evac_flip = [0]
        pending_ln = []

        def flush_ln():
            while pending_ln:
                ti = pending_ln.pop(0)
                ln_tile(ti, ti % 2)

        def evac_copy(out, in_, reads, writes, scale=None):
            evac_flip[0] ^= 1
            if evac_flip[0]:
                if scale is None:
                    return act(out, in_, AF.Copy, reads, writes)
                return act(out, in_, AF.Copy, reads, writes, scale=float(scale))
            if scale is None:
                return vop(lambda e: e.tensor_copy(out=out, in_=in_), reads, writes)
            return vop(lambda e: e.tensor_scalar_mul(out=out, in0=in_, scalar1=float(scale)), reads, writes)

        def dve_copy(out, in_, reads, writes, scale=None):
            if scale is None:
                return vop(lambda e: e.tensor_copy(out=out, in_=in_), reads, writes)
            return vop(lambda e: e.tensor_scalar_mul(out=out, in0=in_, scalar1=float(scale)), reads, writes)

        def ffn_phase(l, j, ln_idx):
            xTg = [carve(k * 8192, 8192, BF16, "p (c t) -> p c t", c=8) for k in range(2)]
            aT = carve(16384, 22528, BF16, "p (f t) -> p f t", f=NFC)
            sil = [carve(38912 + k * 1024, 1024, F32) for k in range(2)]
            NGS = 3
            wgu = [carve(40960 + k * 4096, 4096, BF16, "p (a c f) -> p a c f", a=2, c=8) for k in range(NGS)]
            NDS = 3
            wdb = [carve(53248 + k * 2048, 2048, BF16, "p (f d) -> p f d", f=4) for k in range(NDS)]
            assert 53248 + NDS * 2048 <= ARENA
            for fam, lo, n in (("xTg", 0, 16384), ("aT", 16384, 22528), ("sil", 38912, 2048), ("wgu", 40960, 12288),
                               ("wd", 53248, 6144)):
                sc.reg(fam, lo, n)
            wg_v = wg_d[l, j].rearrange("(c p) f -> p c f", p=128)
            wu_v = wu_d[l, j].rearrange("(c p) f -> p c f", p=128)
            wd_v = wd_d[l, j].rearrange("(f p) d -> p f d", p=128)
            NG = 2
            gu_items = [(g, u) for g in range(NG) for u in range(11)]
            d_units = [(0, 4), (4, 4), (8, 4), (12, 4), (16, 4), (20, 2)]
            d_items = [(g, q, h, k) for g in range(NG) for q in range(2) for h in range(2) for k in range(6)]
            gu_next = [0]
            d_next = [0]

            def gu_load():
                n = gu_next[0]
                if n >= len(gu_items):
                    return
                gu_next[0] += 1
                _, u = gu_items[n]
                s = n % NGS
                dma("pool", wgu[s][:, 0], wg_v[:, :, u * 256:(u + 1) * 256], [], [("wgu", s)])
                dma("pool", wgu[s][:, 1], wu_v[:, :, u * 256:(u + 1) * 256], [], [("wgu", s)])

            def d_load():
                n = d_next[0]
                if n >= len(d_items):
                    return
                d_next[0] += 1
                _, _, h, k = d_items[n]
                f0, nf = d_units[k]
                s = n % NDS
                dma("pool", wdb[s][:, 0:nf, :], wd_v[:, f0:f0 + nf, h * 512:(h + 1) * 512], [], [("wd", s)])
            for _ in range(NGS):
                gu_load()
            for _ in range(NDS):
                d_load()
            gu_i = 0
            d_i = 0

            def one_transpose_group(g, c, th):
                xt = xTg[g % 2]
                xtk = ("xTg", g % 2)
                bank = (2 * c + th) % 4
                for tl in range(4):
                    i = 8 * g + 4 * th + tl
                    tr(psum[bank][:, tl * 128:(tl + 1) * 128], xres[:, i, c * 128:(c + 1) * 128],
                       [("x", i), "ident"], [PK(bank)], signal=(tl == 3))
                act(xt[:, c, th * 512:(th + 1) * 512], psum[bank][:, :], AF.Copy, [PK(bank)], [(xtk, c, th)])

            def group_transposes(g):
                for c in range(8):
                    for th in range(2):
                        one_transpose_group(g, c, th)
            group_transposes(0)
            flush_ln()
            load_gb(l, ln_idx)
            for g in range(NG):
                xt = xTg[g % 2]
                xtk = ("xTg", g % 2)
                for u in range(11):
                    s = gu_i % NGS
                    gu_i += 1
                    for fl in range(2):
                        fc = 2 * u + fl
                        for th in range(2):
                            bg = th + 4 * (fc % 2)
                            bu = 2 + th + 4 * (fc % 2)
                            for c in range(8):
                                mm(psum[bg][:, :], wgu[s][:, 0, c, fl * 128:(fl + 1) * 128], xt[:, c, th * 512:(th + 1) * 512],
                                   c == 0, c == 7, [("wgu", s), (xtk, c, th)], [PK(bg)], signal=(c == 7))
                            for c in range(8):
                                mm(psum[bu][:, :], wgu[s][:, 1, c, fl * 128:(fl + 1) * 128], xt[:, c, th * 512:(th + 1) * 512],
                                   c == 0, c == 7, [("wgu", s), (xtk, c, th)], [PK(bu)], signal=(c == 7))
                            act(sil[th][:, :], psum[bg][:, :], AF.Silu, [PK(bg)], [("sil", th)])
                            vop(lambda e, fc=fc, bu=bu, th=th: e.scalar_tensor_tensor(
                                out=aT[:, fc, th * 512:(th + 1) * 512], in0=sil[th][:, :], scalar=0.5, in1=psum[bu][:, :],
                                op0=ALU.mult, op1=ALU.mult), [("sil", th), PK(bu)], [("aT", fc, th)])
                    gu_load()
                    if pending_ln:
                        ti = pending_ln.pop(0)
                        ln_tile(ti, ti % 2)
                for q in range(2):
                    for h in range(2):
                        bb = 4 if (2 * q + h) % 2 == 0 else 0
                        hoist = (q == 1 and h == 0 and g + 1 < NG)
                        tgroups = [(c, th) for c in range(8) for th in range(2)] if hoist else []
                        for k in range(6):
                            f0, nf = d_units[k]
                            s = d_i % NDS
                            d_i += 1
                            for fl in range(nf):
                                fc = f0 + fl
                                for tl in range(4):
                                    t0 = (4 * q + tl) * 128
                                    mm(psum[bb + tl][:, :], aT[:, fc, t0:t0 + 128], wdb[s][:, fl, :],
                                       fc == 0, fc == NFC - 1, [("wd", s), ("aT", fc, q)], [PK(bb + tl)],
                                       signal=(fc == NFC - 1 or (fl == nf - 1 and tl == 3)))
                            d_load()
                            for _ in range(3):
                                if tgroups:
                                    c, th = tgroups.pop(0)
                                    one_transpose_group(g + 1, c, th)
                        for tl in range(4):
                            resid_add(8 * g + 4 * q + tl, h, bb + tl)
                        for _ in range(2):
                            if pending_ln:
                                ti = pending_ln.pop(0)
                                ln_tile(ti, ti % 2)
                    pending_ln.extend(8 * g + 4 * q + tl for tl in range(4))

        def full_transpose(xT):
            for tb in range(4):
                if tb == 3:
                    flush_ln()
                for c in range(8):
                    bank = (c + 4 * tb) % 4
                    for tl in range(4):
                        i = 4 * tb + tl
                        tr(psum[bank][:, tl * 128:(tl + 1) * 128], xres[:, i, c * 128:(c + 1) * 128],
                           [("x", i), "ident"], [PK(bank)], signal=(tl == 3))
                    evac_copy(xT[:, c, tb * 512:(tb + 1) * 512], psum[bank][:, :], [PK(bank)], [("xT", c, tb)])

        sbank = [0]
        obank = [0]
        ptslot = [0]

        DUMMY_N = 384
        deferred_norm = []

        class Filler:
            def __init__(self, items, pace):
                self.items = list(items)
                self.pace = pace
                self.n = 0

            def __call__(self):
                self.n += 1
                if self.items and self.n % self.pace == 0:
                    self.items.pop(0)()
                    return True
                return False

            def flush(self):
                while self.items:
                    self.items.pop(0)()

        def flush_norm():
            while deferred_norm:
                deferred_norm.pop(0)()

        def attn_core(qt, qk, q0, kt, kk, k0, K, vfn, vk, nkb_fn, causal, out_ap_fn, out_key_fn, PT, rec, filler=None, tok=None):
            identb = ident[:, :].bitcast(BF16)
            for tb in range(4):
                nkb = nkb_fn(tb)
                ob = 3 + obank[0]
                obank[0] ^= 1
                pend = []

                def issue_s(kb):
                    jj = kb - 4 * tb if causal else -1
                    c0 = max(0, jj) * 128
                    b = sbank[0]
                    sbank[0] = (sbank[0] + 1) % 3
                    mm(psum[b][:, c0:512], kt[k0:k0 + K, kb * 128:(kb + 1) * 128],
                       qt[q0:q0 + K, tb * 512 + c0:(tb + 1) * 512], True, True,
                       [kk, qk], [PK(b)], signal=True)
                    did = False
                    if filler is not None and c0 == 0:
                        did = filler()
                    if DUMMY_N and c0 == 0 and not did:
                        n1 = min(DUMMY_N, 256)
                        mm(psum[7][:, 0:n1], tri[:, :], identb[:, 0:n1], True, True, ["tri", "ident"], [PK(7)], signal=False)
                        if DUMMY_N > 256:
                            mm(psum[7][:, 256:DUMMY_N], tri[:, :], identb[:, 0:DUMMY_N - 256], True, True,
                               ["tri", "ident"], [PK(7)], signal=False)
                    sl = ptslot[0]
                    ptslot[0] = (ptslot[0] + 1) % 3
                    act(PT[sl][:, c0:512], psum[b][:, c0:512], AF.Exp, [PK(b)], [("PT", sl)])
                    if jj >= 0:
                        vop(lambda e: e.tensor_tensor(out=PT[sl][:, c0:c0 + 128], in0=PT[sl][:, c0:c0 + 128],
                                                      in1=tri[:, :], op=ALU.mult), [("PT", sl), "tri"], [("PT", sl)],
                            eng="pool")
                    return (kb, sl, c0)

                def issue_pv(item):
                    kb, sl, c0 = item
                    if tok is None:
                        mm(psum[ob][:, c0:512], vfn(kb), PT[sl][:, c0:512], kb == 0, kb == nkb - 1,
                           [vk, ("PT", sl)], [PK(ob)], signal=True)
                        return
                    for sub in range(c0 // 128, 4):
                        mm(psum[ob][:, sub * 65:(sub + 1) * 65], PT[sl][:, sub * 128:(sub + 1) * 128], vfn(kb)[:, 0:65],
                           kb == 0 and sub == 0, kb == nkb - 1 and sub == 3, [vk, ("PT", sl)], [PK(ob)],
                           signal=(sub == 3), sgc=True)
                for kb in range(nkb):
                    pend.append(issue_s(kb))
                    if kb == 1:
                        flush_norm()
                    if len(pend) > 2:
                        issue_pv(pend.pop(0))
                while pend:
                    issue_pv(pend.pop(0))
                oa = out_ap_fn(tb)
                ok = out_key_fn(tb)

                def norm(ob=ob, oa=oa, ok=ok, tb=tb):
                    if tok is None:
                        vop(lambda e: e.reciprocal(out=rec[64:128, :], in_=psum[ob][64:128, :]), [PK(ob)], ["rec"])
                        vop(lambda e: e.tensor_tensor(out=oa, in0=psum[ob][0:64, :], in1=rec[64:128, :], op=ALU.mult),
                            [PK(ob), "rec"], [ok])
                        return
                    mixtok, col0 = tok
                    par = ob % 2
                    ov = psum[ob][:, 0:260].rearrange("p (s e) -> p s e", e=65)
                    vop(lambda e: e.reciprocal(out=rden[:, par, :], in_=ov[:, :, 64]), [PK(ob)], [("rden", par)])
                    for sub in range(4):
                        vop(lambda e, sub=sub: e.tensor_scalar_mul(out=mixtok[:, 4 * tb + sub, col0:col0 + 64],
                                                                    in0=ov[:, sub, 0:64], scalar1=rden[:, par, sub:sub + 1]),
                            [PK(ob), ("rden", par)], [("mixtok", 4 * tb + sub)])
                deferred_norm.append(norm)
            flush_norm()

        def proj_pair(w, wkeys, xT, dst, scale=None, M=128):
            for tb in range(4):
                b = 5 + (tb % 2)
                for c in range(8):
                    mm(psum[b][0:M, :], w[:, c, 0:M], xT[:, c, tb * 512:(tb + 1) * 512], c == 0, c == 7,
                       list(wkeys) + [("xT", c, tb)], [PK(b)], signal=(c == 7))
                for (oap, r0, r1, key) in dst(tb):
                    dve_copy(oap, psum[b][r0:r1, :], [PK(b)], [key], scale=scale)

        def mem_attention(l, xT, mixT, mix_c0, w_in_v, memq_c0, A, mixtok=None):
            wA, wB, kmT, Vm, PT, rec = A["wq"], A["wk"], A["kmT"], A["Vm"], A["PT"], A["rec"]
            qm, qmk = A["qm"], A["qmk"]
            wkv_v = wkv_d[l].rearrange("(c p) f -> p c f", p=128)
            vop(lambda e: e.memset(Vm[:, :, :, 64:128], 1.0), [], ["Vm"], eng="pool")
            for hp in range(2):
                dma("pool", wA[:, :, :], wkv_v[:, :, hp * 128:(hp + 1) * 128], [], ["wq"])
                for c in range(8):
                    mm(psum[7][:, 0:256], wA[:, c, :], memT[:, c, :], c == 0, c == 7, ["wq", "memT"], [PK(7)], signal=(c == 7))
                dve_copy(kmT[:, hp, :], psum[7][:, 0:256], [PK(7)], [("kmT", hp)])
            for vh in range(2):
                dma("pool", wB[:, :, :], wkv_v[:, :, 256 + vh * 128:256 + (vh + 1) * 128], [], ["wk"])
                for mb in range(2):
                    for c in range(8):
                        mm(psum[7][:, 0:128], memT[:, c, mb * 128:(mb + 1) * 128], wB[:, c, :], c == 0, c == 7,
                           ["wk", "memT"], [PK(7)], signal=(c == 7))
                    dve_copy(Vm[:, mb, 2 * vh:2 * vh + 2, 0:64],
                             psum[7][:, 0:128].rearrange("p (h e) -> p h e", h=2), [PK(7)], ["Vm"])
            for hp in range(2):
                dma("pool", wA[:, :, :], w_in_v[:, :, memq_c0 + hp * 128:memq_c0 + (hp + 1) * 128], [], ["wq"])
                proj_pair(wA, ["wq"], xT, lambda tb: [(qm[:, tb * 512:(tb + 1) * 512], 0, 128, qmk)], scale=0.125)
                for s2 in range(2):
                    h = 2 * hp + s2
                    attn_core(qm, qmk, s2 * 64, kmT[:, hp, :], ("kmT", hp), s2 * 64, 64,
                              lambda kb, h=h: Vm[:, kb, h, :], "Vm", lambda tb: 2, False,
                              lambda tb, h=h: mixT[(h % 2) * 64:(h % 2) * 64 + 64, mix_c0 + h // 2, tb * 512:(tb + 1) * 512],
                              lambda tb, h=h: ("mixT", mix_c0 + h // 2, tb), PT, rec,
                              tok=None if mixtok is None else (mixtok, 768 + 64 * h))

        def wo_phase(l, mix_chunks, wo_loader, wo_keys, tokinfo=None):
            flush_ln()
            load_gb(l, 1)
            wo_loader()
            n = len(mix_chunks)

            def prep_tile(i):
                mixtok, mixTt = tokinfo
                tbk = i % 2
                pb = psum[tbk][:, :].bitcast(BF16)
                for c in range(8):
                    tr16(pb[:, c * 128:(c + 1) * 128], mixtok[:, i, c * 128:(c + 1) * 128],
                         [("mixtok", i), "identb16"], [PK(tbk)], signal=(c == 7))
                act(mixTt[i % 2][:, :], pb[:, :], AF.Copy, [PK(tbk)], [("mixTt", i % 2)])
            if tokinfo is not None:
                prep_tile(0)
            for i in range(16):
                if tokinfo is not None and i + 1 < 16:
                    prep_tile(i + 1)
                for h in range(2):
                    b = 4 + 2 * (i % 2) + h
                    for ci, (lf, rf, kf) in enumerate(mix_chunks):
                        mm(psum[b][:, :], lf(i), rf(h), ci == 0, ci == n - 1, list(kf(i)) + list(wo_keys), [PK(b)],
                           signal=(ci == n - 1))
                    resid_add(i, h, b)
                ln_tile(i, i % 2)

        def fox_phase(l, jx):
            xT = carve(0, 16384, BF16, "p (c t) -> p c t", c=8)
            mixT = carve(16384, 16384, BF16, "p (c t) -> p c t", c=8)
            mixtok = carve(16384, 16384, BF16, "p (i c) -> p i c", i=16)
            ptmp = carve(54400, 2048, BF16)
            A = {}
            A["wq"] = carve(32768, 1024, BF16, "p (c f) -> p c f", c=8)
            A["wk"] = carve(33792, 1024, BF16, "p (c f) -> p c f", c=8)
            wv = carve(34816, 1024, BF16, "p (c f) -> p c f", c=8)
            wf = carve(35840, 128, BF16, "p (c f) -> p c f", c=8)
            qa = [[carve(35968 + (2 * p + k) * 2048, 2048, BF16) for k in range(2)] for p in range(2)]
            ka = [[carve(44160 + (2 * p + k) * 2048, 2048, BF16) for k in range(2)] for p in range(2)]
            Vaug = [carve(52352 + p * 4096, 4096, BF16, "p (i s e) -> p i s e", i=16, s=2) for p in range(2)]
            A["PT"] = [carve(60544 + k * 512, 512, BF16) for k in range(3)]
            A["rec"] = carve(62080, 1024, F32)
            pc2 = carve(63104, 2048, BF16)
            dec = carve(56448, 4096, F32)
            A["kmT"] = carve(52352, 512, BF16, "p (h m) -> p h m", h=2)
            A["Vm"] = carve(52864, 1024, BF16, "p (b h e) -> p b h e", b=2, h=4)
            A["qm"], A["qmk"] = qa[0][0], ("qa", 0, 0)
            assert 65152 <= ARENA
            for fam, lo, n in (("xT", 0, 16384), ("mixT", 16384, 16384), ("wq", 32768, 1024), ("wk", 33792, 1024),
                               ("wv", 34816, 1024), ("wf", 35840, 128), ("qa", 35968, 8192), ("ka", 44160, 8192),
                               ("Vaug", 52352, 8192), ("PT", 60544, 1536), ("rec", 62080, 1024), ("dec", 56448, 4096),
                               ("pc", 63104, 2048), ("mixtok", 16384, 16384), ("mixTt", 52352, 2048), ("kmT", 52352, 512), ("Vm", 52864, 1024), ("ptmp", 54400, 2048)):
                sc.reg(fam, lo, n)
            w_in_v = fox_w[jx].rearrange("(c p) f -> p c f", p=128)
            full_transpose(xT)
            mem_attention(l, xT, mixT, 6, w_in_v, 2316, A, mixtok=mixtok)
            wq, wk = A["wq"], A["wk"]
            negb = smallc[0:12, 0:1]
            dma("sp", smallc[0:12, 1:2], fox_b[jx, :].rearrange("(h o) -> h o", o=1), [], ["bf"])
            vop(lambda e: e.tensor_scalar_mul(out=negb, in0=smallc[0:12, 1:2], scalar1=-1.0), ["bf"], ["negb"])
            dma("pool", wf[:, :, 0:12], w_in_v[:, :, 2304:2316], [], ["wf"])
            for tb in range(4):
                for c in range(8):
                    mm(psum[7][0:12, :], wf[:, c, 0:12], xT[:, c, tb * 512:(tb + 1) * 512], c == 0, c == 7,
                       ["wf", ("xT", c, tb)], [PK(7)], signal=(c == 7))
                act(dec[0:12, tb * 512:(tb + 1) * 512], psum[7][0:12, :], AF.Exp, [PK(7), "negb"], ["dec"],
                    bias=negb, scale=-1.0)
            act(dec[0:12, :], dec[0:12, :], AF.Ln, ["dec"], ["dec"], bias=1.0)
            vop(lambda e: e.memset(ptmp[0:12, :], 1.0), [], ["ptmp"])
            vop(lambda e: e.tensor_tensor_scan(out=dec[0:12, :], data0=ptmp[0:12, :], data1=dec[0:12, :],
                                               initial=0.0, op0=ALU.mult, op1=ALU.add), ["dec", "ptmp"], ["dec"])
            for r in range(3):
                vop(lambda e, r=r: e.tensor_copy(out=pc2[32 * r:32 * r + 12, :], in_=dec[0:12, :]), ["dec"], ["pc"])
                if r < 2:
                    vop(lambda e: e.tensor_copy(out=ptmp[0:12, :], in_=dec[0:12, :]), ["dec"], ["ptmp"])
                    vop(lambda e: e.tensor_tensor(out=dec[0:12, :], in0=dec[0:12, :], in1=ptmp[0:12, :],
                                                  op=ALU.subtract), ["dec", "ptmp"], ["dec"])
            for p in range(2):
                for s2 in range(2):
                    vop(lambda e, p=p, s2=s2: e.memset(qa[p][s2][64:70, :], 1.0), [], [("qa", p, s2)])
                    vop(lambda e, p=p, s2=s2: e.memset(ka[p][s2][64:70, :], -1.0), [], [("ka", p, s2)])
                vop(lambda e, p=p: e.memset(Vaug[p][:, :, :, 64:128], 1.0), [], [("Vaug", p)], eng="pool")

            def prep_items(hp):
                p = hp % 2
                items = []

                def loads():
                    dma("pool", wq[:, :, :], w_in_v[:, :, hp * 128:(hp + 1) * 128], [], ["wq"])
                    dma("pool", wk[:, :, :], w_in_v[:, :, 768 + hp * 128:768 + (hp + 1) * 128], [], ["wk"])
                    dma("pool", wv[:, :, :], w_in_v[:, :, 1536 + hp * 128:1536 + (hp + 1) * 128], [], ["wv"])
                    for s2 in range(2):
                        h = 2 * hp + s2
                        for r in range(3):
                            dma("sp", qa[p][s2][64 + r:65 + r, :], pc2[32 * r + h:32 * r + h + 1, :], ["pc"], [("qa", p, s2)])
                            dma("sp", ka[p][s2][67 + r:68 + r, :], pc2[32 * r + h:32 * r + h + 1, :], ["pc"], [("ka", p, s2)])
                items.append(loads)

                def proj_item(w, wkey, dst, dkey, scale, tb):
                    def f():
                        b = 5 + (tb % 2)
                        for c in range(8):
                            mm(psum[b][:, :], w[:, c, :], xT[:, c, tb * 512:(tb + 1) * 512], c == 0, c == 7,
                               [wkey, ("xT", c, tb)], [PK(b)], signal=(c == 7))
                        for s2 in range(2):
                            dve_copy(dst[p][s2][0:64, tb * 512:(tb + 1) * 512], psum[b][s2 * 64:(s2 + 1) * 64, :], [PK(b)],
                                     [(dkey, p, s2)], scale=scale)
                    return f
                for tb in range(4):
                    items.append(proj_item(wq, "wq", qa, "qa", 0.125, tb))
                for tb in range(4):
                    items.append(proj_item(wk, "wk", ka, "ka", None, tb))

                def v_item(q4):
                    def f():
                        b = 5 + (q4 % 2)
                        for tl in range(4):
                            i = 4 * q4 + tl
                            for c in range(8):
                                mm(psum[b][:, tl * 128:(tl + 1) * 128], xT[:, c, i * 128:(i + 1) * 128], wv[:, c, :],
                                   c == 0, c == 7, ["wv", ("xT", c, q4)], [PK(b)], signal=(c == 7 and tl == 3))
                        dve_copy(Vaug[p][:, 4 * q4:4 * q4 + 4, :, 0:64],
                                 psum[b][:, :].rearrange("p (i s e) -> p i s e", i=4, s=2), [PK(b)], [("Vaug", p)])
                    return f
                for q4 in range(4):
                    items.append(v_item(q4))
                return items
            for it in prep_items(0):
                it()
            for hp in range(6):
                p = hp % 2
                fill = Filler(prep_items(hp + 1), 5) if hp + 1 < 6 else None
                for s2 in range(2):
                    h = 2 * hp + s2
                    attn_core(qa[p][s2], ("qa", p, s2), 0, ka[p][s2], ("ka", p, s2), 0, 70,
                              lambda kb, p=p, s2=s2: Vaug[p][:, kb, s2, :], ("Vaug", p), lambda tb: 4 * tb + 4, True,
                              lambda tb, h=h: mixT[(h % 2) * 64:(h % 2) * 64 + 64, h // 2, tb * 512:(tb + 1) * 512],
                              lambda tb, h=h: ("mixT", h // 2, tb), A["PT"], A["rec"], filler=fill, tok=(mixtok, 64 * h))
                if fill is not None:
                    fill.flush()
            wo_sb = carve(35968, 8192, BF16, "p (c d) -> p c d", c=8)
            wo_keys = [("qa", 0, 0), ("qa", 0, 1), ("qa", 1, 0), ("qa", 1, 1)]

            def wo_loader():
                wo_v = wo_d[l].rearrange("(c p) d -> p c d", p=128)
                for hh in range(2):
                    dma("pool", wo_sb[:, 4 * hh:4 * hh + 4, :], wo_v[:, 4 * hh:4 * hh + 4, :], [], wo_keys)
            mixTt = [carve(52352 + k * 1024, 1024, BF16) for k in range(2)]
            chunks = []
            for c in range(8):
                chunks.append((lambda i, c=c: mixTt[i % 2][:, c * 128:(c + 1) * 128],
                               lambda h, c=c: wo_sb[:, c, h * 512:(h + 1) * 512],
                               lambda i, c=c: [("mixTt", i % 2)]))
            wo_phase(l, chunks, wo_loader, wo_keys, tokinfo=(mixtok, mixTt))

        def pool_phase(l, jx):
            xT = carve(0, 16384, BF16, "p (c t) -> p c t", c=8)
            mixT = carve(16384, 20480, BF16, "p (c t) -> p c t", c=10)
            o = 36864
            A = {}
            A["wq"] = carve(o, 1024, BF16, "p (c f) -> p c f", c=8)
            wu = carve(o, 1536, BF16, "p (c f) -> p c f", c=8); o += 1024
            A["wk"] = carve(o, 1024, BF16, "p (c f) -> p c f", c=8); o += 1024
            qa = [carve(o + k * 2048, 2048, BF16) for k in range(2)]; o += 4096
            A["qm"], A["qmk"] = qa[0], ("qa", 0)
            A["PT"] = [carve(o + k * 512, 512, BF16) for k in range(3)]; o += 1536
            A["rec"] = carve(o, 1024, F32); o += 1024
            A["kmT"] = carve(o, 512, BF16, "p (h m) -> p h m", h=2); o += 512
            A["Vm"] = carve(o, 1024, BF16, "p (b h e) -> p b h e", b=2, h=4); o += 1024
            assert o == 47104
            pooled = carve(o, 4096, BF16, "p (k t) -> p k t", k=2)
            uu = carve(o + 4096, 4096, F32)
            Sab = [carve(o + 8192, 4096, F32), carve(o + 12288, 4096, F32)]
            assert o + 16384 <= ARENA
            for fam, lo, n in (("xT", 0, 16384), ("mixT", 16384, 20480), ("wq", 36864, 1024), ("wk", 37888, 1024),
                               ("qa", 38912, 4096), ("PT", 43008, 1536), ("rec", 44544, 1024), ("kmT", 45568, 512),
                               ("Vm", 46080, 1024), ("pooled", 47104, 4096), ("uu", 51200, 4096), ("S", 55296, 8192)):
                sc.reg(fam, lo, n)
            w_in_v = pool_w[jx].rearrange("(c p) f -> p c f", p=128)
            full_transpose(xT)
            mem_attention(l, xT, mixT, 8, w_in_v, 768, A)
            for ch in range(8):
                dma("sp", smallc[0:96, 8 + ch:9 + ch], pool_scale[jx, ch * 96:(ch + 1) * 96].rearrange("(p o) -> p o", o=1),
                    [], ["pscale"])
            for g in range(4):
                w = 2 ** (g + 1)
                dma("pool", wu[:, :, :], w_in_v[:, :, g * 192:(g + 1) * 192], [], ["wq", "wk"])
                dma("pool", wgrp_sb[0:96, :, :], pool_grp[jx, g].rearrange("(k p) o -> p k o", p=96), [], ["wgrp"])
                for kc in range(2):
                    for tb in range(4):
                        b = 5 + (tb % 2)
                        for c in range(8):
                            mm(psum[b][0:96, :], wu[:, c, kc * 96:(kc + 1) * 96], xT[:, c, tb * 512:(tb + 1) * 512],
                               c == 0, c == 7, ["wq", "wk", ("xT", c, tb)], [PK(b)], signal=(c == 7))
                        dve_copy(uu[0:96, tb * 512:(tb + 1) * 512], psum[b][0:96, :], [PK(b)], ["uu"])
                    src, srck = uu, "uu"
                    for k in range(g + 1):
                        sh = 2 ** k
                        dst, dstk = Sab[k % 2], ("S", k % 2)
                        vop(lambda e, src=src, dst=dst, sh=sh: e.tensor_tensor(
                            out=dst[0:96, sh:S], in0=src[0:96, sh:S], in1=src[0:96, 0:S - sh], op=ALU.add),
                            [srck], [dstk])
                        vop(lambda e, src=src, dst=dst, sh=sh: e.tensor_copy(out=dst[0:96, 0:sh], in_=src[0:96, 0:sh]),
                            [srck], [dstk])
                        src, srck = dst, dstk
                    oth, othk = Sab[(g + 1) % 2], ("S", (g + 1) % 2)
                    vop(lambda e, src=src, kc=kc, w=w: e.scalar_tensor_tensor(
                        out=pooled[0:96, kc, :], in0=src[0:96, :], scalar=1.0 / w, in1=uu[0:96, :],
                        op0=ALU.mult, op1=ALU.subtract), [srck, "uu"], [("pooled", kc)])
                    vop(lambda e, src=src, oth=oth, w=w: e.tensor_tensor(
                        out=oth[0:96, 0:w - 1], in0=src[0:96, 0:w - 1], in1=smallc[0:96, 16:16 + w - 1], op=ALU.mult),
                        [srck, "rinv"], [othk])
                    vop(lambda e, oth=oth, kc=kc, w=w: e.tensor_tensor(
                        out=pooled[0:96, kc, 0:w - 1], in0=oth[0:96, 0:w - 1], in1=uu[0:96, 0:w - 1], op=ALU.subtract),
                        [othk, "uu"], [("pooled", kc)])
                for oc in range(2):
                    for tb in range(4):
                        b = 5 + (tb % 2)
                        for kc in range(2):
                            mm(psum[b][0:96, :], wgrp_sb[0:96, kc, oc * 96:(oc + 1) * 96],
                               pooled[0:96, kc, tb * 512:(tb + 1) * 512], kc == 0, kc == 1,
                               ["wgrp", ("pooled", kc)], [PK(b)], signal=(kc == 1))
                        ch = 2 * g + oc
                        vop(lambda e, b=b, ch=ch, tb=tb: e.tensor_scalar_mul(
                            out=mixT[0:96, ch, tb * 512:(tb + 1) * 512], in0=psum[b][0:96, :],
                            scalar1=smallc[0:96, 8 + ch:9 + ch]), [PK(b), "pscale"], [("mixT", ch, tb)])
            wo_p = carve(47104, 8192, BF16, "p (c d) -> p c d", c=8)
            wo_m = carve(47104 + 8192, 2048, BF16, "p (c d) -> p c d", c=2)
            wo_keys = [("pooled", 0), ("pooled", 1), "uu", ("S", 0), ("S", 1)]

            def wo_loader():
                dma("pool", wo_p[0:96, :, :], wo_d[l][0:768, :].rearrange("(c p) d -> p c d", p=96), [], wo_keys)
                dma("pool", wo_m[:, :, :], wo_d[l][768:1024, :].rearrange("(c p) d -> p c d", p=128), [], wo_keys)
            chunks = []
            for c in range(8):
                chunks.append((lambda i, c=c: mixT[0:96, c, i * 128:(i + 1) * 128],
                               lambda h, c=c: wo_p[0:96, c, h * 512:(h + 1) * 512],
                               lambda i, c=c: [("mixT", c, i // 4)]))
            for c in range(2):
                chunks.append((lambda i, c=c: mixT[:, 8 + c, i * 128:(i + 1) * 128],
                               lambda h, c=c: wo_m[:, c, h * 512:(h + 1) * 512],
                               lambda i, c=c: [("mixT", 8 + c, i // 4)]))
            wo_phase(l, chunks, wo_loader, wo_keys)

        def mla_phase(l, jx):
            xT = carve(0, 16384, BF16, "p (c t) -> p c t", c=8)
            mixT = carve(16384, 16384, BF16, "p (c t) -> p c t", c=8)
            mixtok = carve(16384, 16384, BF16, "p (i c) -> p i c", i=16)
            o = 32768
            A = {}
            wreg = o
            A["wq"] = carve(o, 1024, BF16, "p (c f) -> p c f", c=8); o += 1024
            A["wk"] = carve(o, 1024, BF16, "p (c f) -> p c f", c=8); o += 1024
            qa = [carve(o + k * 2048, 2048, BF16) for k in range(2)]; o += 4096
            ka = [carve(o + k * 2048, 2048, BF16) for k in range(2)]; o += 4096
            A["qm"], A["qmk"] = qa[0], ("qa", 0)
            A["PT"] = [carve(o + k * 512, 512, BF16) for k in range(3)]; o += 1536
            A["rec"] = carve(o, 1024, F32); o += 1024
            A["kmT"] = carve(o, 512, BF16, "p (h m) -> p h m", h=2); o += 512
            A["Vm"] = carve(o, 1024, BF16, "p (b h e) -> p b h e", b=2, h=4); o += 1024
            assert o == 47104
            cqT = carve(o, 6144, BF16, "p (c t) -> p c t", c=3)
            angf = carve(o, 4096, F32)
            o += 6144
            ckvT = carve(o, 4096, BF16, "p (c t) -> p c t", c=2); o += 4096
            Vaug = carve(o, 4096, BF16, "p (i s e) -> p i s e", i=16, s=2)
            posi = carve(o, 4096, F32).bitcast(I32)
            posf = carve(o, 4096, F32)
            o += 4096
            t1 = carve(o, 1024, F32); t2 = carve(o + 1024, 1024, F32); o += 2048
            sqb = carve(o - 1024, 512, BF16)
            assert o <= ARENA
            Ct, Sn = qa[0], ka[0]
            for fam, lo, n in (("xT", 0, 16384), ("mixT", 16384, 16384), ("wq", 32768, 1024), ("wk", 33792, 1024),
                               ("qa", 34816, 4096), ("ka", 38912, 4096), ("CtT", 34816, 2048), ("SnT", 38912, 2048),
                               ("PT", 43008, 1536), ("rec", 44544, 1024), ("kmT", 45568, 512), ("Vm", 46080, 1024), ("cq", 47104, 6144), ("ang", 47104, 4096),
                               ("ckv", 53248, 4096), ("kf", 53248, 4096), ("Vaug", 57344, 4096), ("mixtok", 16384, 16384), ("mixTt", 57344, 2048), ("t1", 61440, 1024),
                               ("t2", 62464, 1024), ("sqb", 62464, 512)):
                sc.reg(fam, lo, n)
            w_in_v = mla_w[jx].rearrange("(c p) f -> p c f", p=128)
            full_transpose(xT)
            mem_attention(l, xT, mixT, 6, w_in_v, 672, A, mixtok=mixtok)
            wq, wk = A["wq"], A["wk"]
            invf = smallc[64:96, 32:33]
            vop(lambda e: e.iota(smallc[64:96, 33:34], pattern=[[0, 1]], base=0, channel_multiplier=1,
                                 allow_small_or_imprecise_dtypes=True), [], ["invf"], eng="pool")
            vop(lambda e: e.tensor_single_scalar(out=smallc[64:96, 34:35], in_=smallc[64:96, 33:34], scalar=16.0,
                                                 op=ALU.is_ge), ["invf"], ["invf2"])
            vop(lambda e: e.scalar_tensor_tensor(out=smallc[64:96, 33:34], in0=smallc[64:96, 34:35], scalar=-16.0,
                                                 in1=smallc[64:96, 33:34], op0=ALU.mult, op1=ALU.add),
                ["invf", "invf2"], ["invf"])
            act(invf, smallc[64:96, 33:34], AF.Exp, ["invf"], ["invf"], scale=-math.log(10000.0) / 16.0)
            dma("sp", posi[64:96, :], pos_d[0, :].partition_broadcast(32), [], ["Vaug"])
            vop(lambda e: e.tensor_copy(out=posf[64:96, :], in_=posi[64:96, :]), ["Vaug"], ["Vaug"])
            kff = carve(47104 + 6144, 4096, F32)
            kfi = carve(47104 + 6144, 4096, F32).bitcast(I32)
            TWO_PI = float(2 * math.pi)
            for (tab, tk, shift) in ((Sn, "SnT", 0.0), (Ct, "CtT", 0.5 * math.pi)):
                vop(lambda e, shift=shift: e.tensor_scalar(out=angf[64:96, :], in0=posf[64:96, :], scalar1=invf,
                                                           scalar2=float(shift), op0=ALU.mult, op1=ALU.add),
                    ["Vaug", "invf"], ["ang"])
                vop(lambda e: e.tensor_scalar(out=kfi[64:96, :], in0=angf[64:96, :], scalar1=float(1.0 / TWO_PI),
                                              scalar2=None, op0=ALU.mult), ["ang"], ["kf"])
                vop(lambda e: e.tensor_copy(out=kff[64:96, :], in_=kfi[64:96, :]), ["kf"], ["kf"])
                vop(lambda e: e.scalar_tensor_tensor(out=angf[64:96, :], in0=kff[64:96, :], scalar=-TWO_PI,
                                                     in1=angf[64:96, :], op0=ALU.mult, op1=ALU.add), ["kf", "ang"], ["ang"])
                vop(lambda e: e.tensor_single_scalar(out=kff[64:96, :], in_=angf[64:96, :], scalar=float(math.pi),
                                                     op=ALU.is_gt), ["ang", "kf"], ["kf"])
                vop(lambda e: e.scalar_tensor_tensor(out=angf[64:96, :], in0=kff[64:96, :], scalar=-TWO_PI,
                                                     in1=angf[64:96, :], op0=ALU.mult, op1=ALU.add), ["kf", "ang"], ["ang"])
                vop(lambda e: e.tensor_scalar(out=angf[64:96, :], in0=angf[64:96, :], scalar1=float(math.pi),
                                              scalar2=float(-math.pi), op0=ALU.min, op1=ALU.max), ["ang"], ["ang"])
                act(tab[96:128, :], angf[64:96, :], AF.Sin, ["ang"], [tk])
            gq = smallc[:, 36:39]
            gkv = smallc[:, 40:42]
            for c in range(3):
                dma("sp", smallc[:, 36 + c:37 + c], mla_qn[jx, c * 128:(c + 1) * 128].rearrange("(p o) -> p o", o=1), [], ["gq"])
            for c in range(2):
                dma("sp", smallc[:, 40 + c:41 + c], mla_kvn[jx, c * 128:(c + 1) * 128].rearrange("(p o) -> p o", o=1), [], ["gq"])
            wcq = carve(wreg, 2048, BF16, "p (c f) -> p c f", c=8)
            for (c0, nch, dstT, dk, nfeat, ck) in ((0, 3, cqT, "cq", 384, 0), (384, 2, ckvT, "ckv", 256, 1)):
                for tb in range(4):
                    banks = []
                    for cc in range(nch):
                        if tb == 0 or True:
                            dma("pool", wcq[:, :, 0:128], w_in_v[:, :, c0 + cc * 128:c0 + (cc + 1) * 128], [], ["wq", "wk"])
                        b = cc
                        banks.append(b)
                        for c in range(8):
                            mm(psum[b][:, :], wcq[:, c, 0:128], xT[:, c, tb * 512:(tb + 1) * 512], c == 0, c == 7,
                               ["wq", "wk", ("xT", c, tb)], [PK(b)], signal=(c == 7))
                    for cc in range(nch):
                        act(sqb[:, :], psum[banks[cc]][:, :], AF.Square, [PK(banks[cc])], ["sqb"])
                        mm(psum[4][:, :], onesb[:, :], sqb[:, :], cc == 0, cc == nch - 1, ["onesb", "sqb"], [PK(4)], signal=True)
                    act(t1[:, :], psum[4][:, :], AF.Sqrt, [PK(4), "eps"], ["t1"], bias=epst[:, 1:2], scale=1.0 / nfeat)
                    vop(lambda e: e.reciprocal(out=t1[:, :], in_=t1[:, :]), ["t1"], ["t1"])
                    for cc in range(nch):
                        vop(lambda e, cc=cc, b=banks[cc], tb=tb, dstT=dstT: e.tensor_tensor(
                            out=dstT[:, cc, tb * 512:(tb + 1) * 512], in0=psum[b][:, :], in1=t1[:, :], op=ALU.mult),
                            [PK(banks[cc]), "t1"], [(dk, cc, tb)])
            wkr = carve(wreg, 768, BF16, "p (c f) -> p c f", c=8)
            wkr2 = carve(wreg + 768, 768, BF16, "p (c f) -> p c f", c=8)
            wraw = carve(wreg + 1536, 256, BF16, "p (c f) -> p c f", c=8)
            dma("pool", wraw[:, :, :], w_in_v[:, :, 640:672], [], ["wq", "wk"])
            vop(lambda e: e.memset(wkr[:, :, :], 0.0), ["wq", "wk"], ["wq", "wk"])
            vop(lambda e: e.memset(wkr2[:, :, :], 0.0), ["wq", "wk"], ["wq", "wk"])
            vop(lambda e: e.tensor_copy(out=wkr[:, :, 64:96], in_=wraw[:, :, :]), ["wq", "wk"], ["wq", "wk"])
            vop(lambda e: e.tensor_scalar_mul(out=wkr2[:, :, 64:80], in0=wraw[:, :, 16:32], scalar1=-1.0), ["wq", "wk"], ["wq", "wk"])
            vop(lambda e: e.tensor_copy(out=wkr2[:, :, 80:96], in_=wraw[:, :, 0:16]), ["wq", "wk"], ["wq", "wk"])

            def rope_combine(dst, dkey, bA, bB, tb):
                vop(lambda e: e.tensor_tensor(out=t1[64:96, :], in0=psum[bB][64:96, :], in1=Sn[96:128, tb * 512:(tb + 1) * 512],
                                              op=ALU.mult), [PK(bB), "SnT"], ["t1"])
                vop(lambda e: e.tensor_tensor(out=t2[64:96, :], in0=psum[bA][64:96, :], in1=Ct[96:128, tb * 512:(tb + 1) * 512],
                                              op=ALU.mult), [PK(bA), "CtT"], ["t2"])
                vop(lambda e: e.tensor_tensor(out=dst[64:96, tb * 512:(tb + 1) * 512], in0=t1[64:96, :], in1=t2[64:96, :],
                                              op=ALU.add), ["t1", "t2"], [dkey])
            for tb in range(4):
                for (w_, b) in ((wkr, 5), (wkr2, 6)):
                    for c in range(8):
                        mm(psum[b][0:96, :], w_[:, c, :], xT[:, c, tb * 512:(tb + 1) * 512], c == 0, c == 7,
                           ["wq", "wk", ("xT", c, tb)], [PK(b)], signal=(c == 7))
                rope_combine(ka[0], ("ka", 0), 5, 6, tb)
                vop(lambda e, tb=tb: e.tensor_copy(out=ka[1][64:96, tb * 512:(tb + 1) * 512],
                                                   in_=ka[0][64:96, tb * 512:(tb + 1) * 512]), [("ka", 0)], [("ka", 1)])
            stg_q = carve(wreg, 576, F32, "p (c f) -> p c f", c=3)
            stg_kv = carve(wreg + 576, 512, F32, "p (c f) -> p c f", c=2)
            wuq_h = carve(wreg + 1088, 288, BF16, "p (c f) -> p c f", c=3)
            wrot_h = carve(wreg + 1376, 288, BF16, "p (c f) -> p c f", c=3)
            wukv_h = carve(wreg + 1664, 256, BF16, "p (c f) -> p c f", c=2)
            uq_v = mla_uq[jx].rearrange("(c p) f -> p c f", p=128)
            ukv_v = mla_ukv[jx].rearrange("(c p) f -> p c f", p=128)
            WK = ["wq", "wk"]
            qscale = 96.0 ** -0.5
            vop(lambda e: e.memset(Vaug[:, :, :, 64:128], 1.0), ["Vaug"], ["Vaug", ("Vaug", 0), ("Vaug", 1)], eng="pool")

            def prep_items(h):
                s2 = h % 2
                qt, qk_, kt_, kk_ = qa[s2], ("qa", s2), ka[s2], ("ka", s2)
                items = []

                def loads():
                    dma("sp", stg_q[:, :, :], uq_v[:, :, h * 96:(h + 1) * 96], [], WK)
                    dma("sp", stg_kv[:, :, :], ukv_v[:, :, h * 128:(h + 1) * 128], [], WK)
                    for c in range(3):
                        vop(lambda e, c=c: e.tensor_scalar(out=wuq_h[:, c, :], in0=stg_q[:, c, :], scalar1=smallc[:, 36 + c:37 + c],
                                                           scalar2=float(qscale), op0=ALU.mult, op1=ALU.mult), WK + ["gq"], WK)
                    for c in range(2):
                        vop(lambda e, c=c: e.tensor_scalar_mul(out=wukv_h[:, c, :], in0=stg_kv[:, c, :],
                                                               scalar1=smallc[:, 40 + c:41 + c]), WK + ["gq"], WK)
                    vop(lambda e: e.memset(wrot_h[:, :, 0:64], 0.0), WK, WK)
                    vop(lambda e: e.tensor_scalar_mul(out=wrot_h[:, :, 64:80], in0=wuq_h[:, :, 80:96], scalar1=-1.0), WK, WK)
                    vop(lambda e: e.tensor_copy(out=wrot_h[:, :, 80:96], in_=wuq_h[:, :, 64:80]), WK, WK)
                items.append(loads)

                def qk_item(tb):
                    def f():
                        for (w_, b) in ((wuq_h, 5), (wrot_h, 6)):
                            for c in range(3):
                                mm(psum[b][0:96, :], w_[:, c, :], cqT[:, c, tb * 512:(tb + 1) * 512], c == 0, c == 2,
                                   WK + [("cq", c, tb)], [PK(b)], signal=(c == 2))
                        dve_copy(qt[0:64, tb * 512:(tb + 1) * 512], psum[5][0:64, :], [PK(5)], [qk_])
                        rope_combine(qt, qk_, 5, 6, tb)
                        for c in range(2):
                            mm(psum[5][0:64, :], wukv_h[:, c, 0:64], ckvT[:, c, tb * 512:(tb + 1) * 512], c == 0, c == 1,
                               WK + [("ckv", c, tb)], [PK(5)], signal=(c == 1))
                        dve_copy(kt_[0:64, tb * 512:(tb + 1) * 512], psum[5][0:64, :], [PK(5)], [kk_])
                    return f
                for tb in range(4):
                    items.append(qk_item(tb))

                def v_item(q8):
                    def f():
                        b = 5 + q8
                        for tl in range(8):
                            i = 8 * q8 + tl
                            for c in range(2):
                                mm(psum[b][:, tl * 64:(tl + 1) * 64], ckvT[:, c, i * 128:(i + 1) * 128], wukv_h[:, c, 64:128],
                                   c == 0, c == 1, WK + [("ckv", c, i // 4)], [PK(b)], signal=(c == 1 and tl == 7))
                        dve_copy(Vaug[:, 8 * q8:8 * q8 + 8, s2, 0:64], psum[b][:, :].rearrange("p (i e) -> p i e", i=8),
                                 [PK(b)], [("Vaug", s2)])
                    return f
                for q8 in range(2):
                    items.append(v_item(q8))
                return items
            for it in prep_items(0):
                it()
            for h in range(12):
                s2 = h % 2
                fill = Filler(prep_items(h + 1), 5) if h + 1 < 12 else None
                attn_core(qa[s2], ("qa", s2), 0, ka[s2], ("ka", s2), 0, 96,
                          lambda kb, s2=s2: Vaug[:, kb, s2, :], ("Vaug", s2), lambda tb: 4 * tb + 4, True,
                          lambda tb, h=h: mixT[(h % 2) * 64:(h % 2) * 64 + 64, h // 2, tb * 512:(tb + 1) * 512],
                          lambda tb, h=h: ("mixT", h // 2, tb), A["PT"], A["rec"], filler=fill, tok=(mixtok, 64 * h))
                if fill is not None:
                    fill.flush()
            wo_sb = carve(34816, 8192, BF16, "p (c d) -> p c d", c=8)
            wo_keys = [("qa", 0), ("qa", 1), ("ka", 0), ("ka", 1)]

            def wo_loader():
                wo_v = wo_d[l].rearrange("(c p) d -> p c d", p=128)
                for hh in range(2):
                    dma("pool", wo_sb[:, 4 * hh:4 * hh + 4, :], wo_v[:, 4 * hh:4 * hh + 4, :], [], wo_keys)
            mixTt = [carve(57344 + k * 1024, 1024, BF16) for k in range(2)]
            chunks = []
            for c in range(8):
                chunks.append((lambda i, c=c: mixTt[i % 2][:, c * 128:(c + 1) * 128],
                               lambda h, c=c: wo_sb[:, c, h * 512:(h + 1) * 512],
                               lambda i, c=c: [("mixTt", i % 2)]))
            wo_phase(l, chunks, wo_loader, wo_keys, tokinfo=(mixtok, mixTt))

        for l in range(n_layers):
            kind = l % 3
            ffn_phase(l, 0, 0)
            if kind == 0:
                fox_phase(l, l // 3)
            elif kind == 1:
                pool_phase(l, l // 3)
            else:
                mla_phase(l, l // 3)
            ffn_phase(l, 1, 2)
        flush_ln()
        sc.barrier()
        fin = []
        for q4 in range(4):
            fin.append(dma("sp", out_d[q4 * 512:(q4 + 1) * 512, :].rearrange("(i p) d -> p i d", p=128),
                           xres[:, 4 * q4:4 * q4 + 4, :], [("x", 4 * q4 + k) for k in range(4)], []))
        final_waits = dict()
        for s_, v_ in fin:
            final_waits[s_] = max(final_waits.get(s_, 0), v_)

        with nc.Block() as block:
            def replay(eng_name, eng):
                for waits, fn, inc in sc.prog[eng_name]:
                    for s_, v_ in waits:
                        eng.wait_ge(sems[s_], v_)
                    ins = fn(eng)
                    if inc is not None:
                        ins.then_inc(sems[inc[0]], inc[1])

            @block.sync
            def _(e):
                replay("sp", e)
                for s_, v_ in final_waits.items():
                    e.wait_ge(sems[s_], v_)

            @block.tensor
            def _(e):
                replay("pe", e)

            @block.scalar
            def _(e):
                replay("act", e)

            @block.vector
            def _(e):
                replay("dve", e)

            @block.gpsimd
            def _(e):
                replay("pool", e)
    return nc


_W_NAMES = ["ln_g", "ln_b", "ffn_w_gate", "ffn_w_up", "ffn_w_down", "mem_w_kv", "w_o", "fox_w_in", "fox_b_f",
            "pool_w_in", "pool_w_grp", "pool_scale", "mla_w_in", "mla_q_norm", "mla_kv_norm", "mla_w_uq", "mla_w_ukv"]


def kernel(**inputs):
    n_cores = 8
    nc = build_program()
    x = np.ascontiguousarray(np.asarray(inputs["x"], dtype=np.float32))
    mem = np.asarray(inputs["mem"], dtype=np.float32)
    pos = np.asarray(inputs["positions"], dtype=np.int32)
    shared = {k: np.ascontiguousarray(np.asarray(inputs[k], dtype=np.float32)) for k in _W_NAMES}
    in_maps = []
    for b in range(n_cores):
        m = dict(shared)
        m["x"] = x[b]
        m["memT"] = np.ascontiguousarray(mem[b].T)
        m["positions"] = np.ascontiguousarray(pos[b].reshape(1, S))
        in_maps.append(m)
    res = run_bass_kernel_spmd(nc, in_maps, core_ids=list(range(n_cores)))
    return np.stack([np.asarray(r["out"], dtype=np.float32) for r in res.results], axis=0)
```

```python
import math
from contextlib import ExitStack
import numpy as np
import concourse.bass as bass
import concourse.mybir as mybir
from concourse.bass_utils import run_bass_kernel_spmd

F32 = mybir.dt.float32
BF16 = mybir.dt.bfloat16
I32 = mybir.dt.int32
AF = mybir.ActivationFunctionType
ALU = mybir.AluOpType

S = 2048
D = 1024
DFF = 2816
NFC = 22
DEPTH = 4
ALPHA = (2 * DEPTH) ** 0.25
LN_EPS = 1e-5
RMS_EPS = 1e-6
ENGS = ["pe", "act", "dve", "pool", "sp"]
NDMA_SEM = {"sp": 16, "pool": 6}


class Sched:
    def __init__(self):
        self.prog = {e: [] for e in ENGS}
        self.count = {}
        self.last_w = {}
        self.readers = {}
        self.seen = {e: {} for e in ENGS}
        self.pe_pr = set()
        self.pe_pw = set()
        self.dma_rr = {"sp": 0, "pool": 0}
        self.pending_bar = {e: None for e in ENGS}
        self.fam_iv = {}
        self.fam_state = {}

    def reg(self, fam, lo, n):
        self.fam_iv[fam] = (lo, lo + n)

    def _fam_update(self, reads, writes, tok):
        for k in set(reads) | set(writes):
            fam = k if isinstance(k, str) else k[0]
            iv = self.fam_iv.get(fam)
            if iv is None:
                continue
            st = self.fam_state.setdefault((fam, iv[0], iv[1]), {"w": {}, "r": {}})
            d = st["w"] if k in writes else st["r"]
            if d.get(tok[0], 0) < tok[1]:
                d[tok[0]] = tok[1]

    def _deps(self, reads, writes):
        deps = {}

        def add(tok):
            if tok is None:
                return
            s, v = tok
            if deps.get(s, 0) < v:
                deps[s] = v
        for k in reads:
            add(self.last_w.get(k))
        for k in writes:
            add(self.last_w.get(k))
            for s, v in self.readers.get(k, {}).items():
                add((s, v))
        wset = set(writes)
        for k in set(reads) | wset:
            fam = k if isinstance(k, str) else k[0]
            iv = self.fam_iv.get(fam)
            if iv is None:
                continue
            for (f2, lo2, hi2), st in self.fam_state.items():
                if f2 == fam and lo2 == iv[0] and hi2 == iv[1]:
                    continue
                if lo2 < iv[1] and iv[0] < hi2:
                    for s, v in st["w"].items():
                        add((s, v))
                    if k in wset:
                        for s, v in st["r"].items():
                            add((s, v))
        return deps

    def emit(self, eng, fn, reads=(), writes=(), signal=True, dma=False):
        deps = self._deps(reads, writes)
        bar = self.pending_bar[eng]
        if bar is not None:
            for s, v in bar.items():
                if deps.get(s, 0) < v:
                    deps[s] = v
            self.pending_bar[eng] = None
        waits = []
        for s, v in deps.items():
            if s == "pe" and eng == "pe":
                continue
            if self.seen[eng].get(s, 0) >= v:
                continue
            self.seen[eng][s] = v
            waits.append((s, v))
        inc = None
        tok = None
        if dma:
            i = self.dma_rr[eng]
            self.dma_rr[eng] = (i + 1) % NDMA_SEM[eng]
            sem = "d%s%d" % (eng, i)
            prev = self.count.get(sem, 0)
            if prev > 0 and self.seen[eng].get(sem, 0) < prev:
                self.seen[eng][sem] = prev
                waits.append((sem, prev))
            self.count[sem] = prev + 16
            tok = (sem, self.count[sem])
            inc = (sem, 16)
        elif signal:
            self.count[eng] = self.count.get(eng, 0) + 1
            tok = (eng, self.count[eng])
            inc = (eng, 1)
        self.prog[eng].append((waits, fn, inc))
        if eng == "pe" and not signal:
            self.pe_pr.update(reads)
            self.pe_pw.update(writes)
            return None
        if eng == "pe":
            reads = set(reads) | self.pe_pr
            writes = set(writes) | self.pe_pw
            self.pe_pr = set()
            self.pe_pw = set()
        self._fam_update(reads, writes, tok)
        for k in writes:
            self.last_w[k] = tok
            self.readers[k] = {}
        for k in reads:
            if k in writes:
                continue
            r = self.readers.setdefault(k, {})
            if r.get(tok[0], 0) < tok[1]:
                r[tok[0]] = tok[1]
        return tok

    def barrier(self):
        snap = dict(self.count)
        for e in ENGS:
            self.pending_bar[e] = dict(snap)


def build_program(n_layers=DEPTH):
    nc = bass.Bass("TRN2", target_bir_lowering=False)
    dr = {}

    def din(name, shape, dt=F32):
        dr[name] = nc.dram_tensor(name, list(shape), dt, kind="ExternalInput").ap()
        return dr[name]
    x_d = din("x", [S, D])
    memT_d = din("memT", [D, 256])
    pos_d = din("positions", [1, S], I32)
    ln_g = din("ln_g", [4, 3, D])
    ln_b = din("ln_b", [4, 3, D])
    wg_d = din("ffn_w_gate", [4, 2, D, DFF])
    wu_d = din("ffn_w_up", [4, 2, D, DFF])
    wd_d = din("ffn_w_down", [4, 2, DFF, D])
    wkv_d = din("mem_w_kv", [4, D, 512])
    wo_d = din("w_o", [4, D, D])
    fox_w = din("fox_w_in", [2, D, 2572])
    fox_b = din("fox_b_f", [2, 12])
    pool_w = din("pool_w_in", [1, D, 1024])
    pool_grp = din("pool_w_grp", [1, 4, 192, 192])
    pool_scale = din("pool_scale", [1, 768])
    mla_w = din("mla_w_in", [1, D, 928])
    mla_qn = din("mla_q_norm", [1, 384])
    mla_kvn = din("mla_kv_norm", [1, 256])
    mla_uq = din("mla_w_uq", [1, 384, 1152])
    mla_ukv = din("mla_w_ukv", [1, 256, 1536])
    out_d = nc.dram_tensor("out", [S, D], F32, kind="ExternalOutput").ap()

    sc = Sched()
    es = ExitStack()
    with es:
        def sb(name, shape, dt):
            return es.enter_context(nc.sbuf_tensor(name, list(shape), dt))
        xres = sb("xres", [128, 16, D], F32)
        ident = sb("ident", [128, 128], F32)
        tri = sb("tri", [128, 128], BF16)
        memT = sb("memT_sb", [128, 8, 256], BF16)
        gb = sb("gb", [128, 2, D], F32)
        epst = sb("epst", [128, 2], F32)
        st6 = sb("st6", [128, 2, 2, 6], F32)
        mv = sb("mv", [128, 2, 2], F32)
        rstd = sb("rstd", [128, 2, 4], F32)
        smallc = sb("smallc", [128, 64], F32)
        wgrp_sb = sb("wgrp_sb", [128, 2, 192], BF16)
        onesb = sb("onesb", [128, 128], BF16)
        identb16 = sb("identb16", [128, 128], BF16)
        rden = sb("rden", [128, 2, 4], F32)
        ARENA = 65536
        arena = sb("arena", [128, ARENA], BF16)
        psum = [es.enter_context(nc.psum_tensor("ps%d" % i, [128, 512], F32)) for i in range(8)]
        sems = {}
        for e in ENGS:
            sems[e] = es.enter_context(nc.semaphore("s_" + e))
        for q in ("sp", "pool"):
            for i in range(NDMA_SEM[q]):
                n = "d%s%d" % (q, i)
                sems[n] = es.enter_context(nc.semaphore(n))

        def carve(off, n_elems_bf16, dt, pattern=None, **kw):
            v = arena[:, off:off + n_elems_bf16]
            if dt == F32:
                v = v.bitcast(F32)
            if pattern:
                v = v.rearrange(pattern, **kw)
            return v

        def PK(b):
            return ("ps", b)

        def dma(q, out, in_, reads, writes):
            return sc.emit(q, lambda e: e.dma_start(out=out, in_=in_), reads, writes, dma=True)

        def mm(out, lhsT, rhs, start, stop, reads, writes, signal, sgc=False):
            if sgc:
                return sc.emit("pe", lambda e: e.matmul(out, lhsT, rhs, start=start, stop=stop, skip_group_check=True),
                               reads, writes, signal=signal)
            return sc.emit("pe", lambda e: e.matmul(out, lhsT, rhs, start=start, stop=stop),
                           reads, writes, signal=signal)

        def tr16(out, in_, reads, writes, signal):
            return sc.emit("pe", lambda e: e.transpose(out, in_, identb16[:]), reads, writes, signal=signal)

        def tr(out, in_, reads, writes, signal):
            return sc.emit("pe", lambda e: e.transpose(out, in_, ident[:]), reads, writes, signal=signal)

        def act(out, in_, func, reads, writes, bias=None, scale=None):
            kw = {}
            if bias is not None:
                kw["bias"] = bias
            if scale is not None:
                kw["scale"] = scale
            return sc.emit("act", lambda e: e.activation(out=out, in_=in_, func=func, **kw), reads, writes)

        def vop(fn, reads, writes, eng="dve"):
            return sc.emit(eng, fn, reads, writes)

        vop(lambda e: e.iota(ident[:], pattern=[[1, 128]], base=0, channel_multiplier=-1,
                             allow_small_or_imprecise_dtypes=True), [], ["ident"], eng="pool")
        vop(lambda e: e.tensor_single_scalar(out=tri[:], in_=ident[:], scalar=0.0, op=ALU.is_ge), ["ident"], ["tri"])
        vop(lambda e: e.tensor_single_scalar(out=ident[:], in_=ident[:], scalar=0.0, op=ALU.is_equal), ["ident", "tri"], ["ident"])
        vop(lambda e: e.memset(epst[:, 0:1], LN_EPS), [], ["eps"])
        vop(lambda e: e.memset(onesb[:, :], 1.0), [], ["onesb"])
        vop(lambda e: e.tensor_copy(out=identb16[:, :], in_=ident[:, :]), ["ident"], ["identb16"])
        vop(lambda e: e.iota(smallc[:, 16:32], pattern=[[1, 16]], base=1, channel_multiplier=0,
                             allow_small_or_imprecise_dtypes=True), [], ["rinv"], eng="pool")
        vop(lambda e: e.reciprocal(out=smallc[:, 16:32], in_=smallc[:, 16:32]), ["rinv"], ["rinv"])
        vop(lambda e: e.memset(epst[:, 1:2], RMS_EPS), ["eps"], ["eps"])
        dma("pool", memT[:], memT_d.rearrange("(c p) m -> p c m", p=128), [], ["memT"])
        for q4 in range(4):
            dma("sp", xres[:, 4 * q4:4 * q4 + 4, :],
                x_d[q4 * 512:(q4 + 1) * 512, :].rearrange("(i p) d -> p i d", p=128),
                [], [("x", 4 * q4 + k) for k in range(4)])

        def ln_tile(i, par):
            xk = ("x", i)
            for h in range(2):
                vop(lambda e, h=h: e.bn_stats(out=st6[:, par, h, :], in_=xres[:, i, h * 512:(h + 1) * 512]),
                    [xk], [("st6", par, h)])
            vop(lambda e: e.bn_aggr(out=mv[:, par, :], in_=st6[:, par].rearrange("p a b -> p (a b)")),
                [("st6", par, 0), ("st6", par, 1)], [("mv", par)])
            act(rstd[:, par, 0:1], mv[:, par, 1:2], AF.Sqrt, [("mv", par), "eps"], [("rs0", par)], bias=epst[:, 0:1])
            vop(lambda e: e.reciprocal(out=rstd[:, par, 1:2], in_=rstd[:, par, 0:1]), [("rs0", par)], [("rs1", par)])
            vop(lambda e: e.tensor_scalar(out=rstd[:, par, 2:3], in0=mv[:, par, 0:1], scalar1=rstd[:, par, 1:2],
                                          scalar2=-1.0, op0=ALU.mult, op1=ALU.mult),
                [("mv", par), ("rs1", par)], [("nmr", par)])
            act(xres[:, i, :], xres[:, i, :], AF.Identity, [xk, ("rs1", par), ("nmr", par)], [xk],
                bias=rstd[:, par, 2:3], scale=rstd[:, par, 1:2])
            vop(lambda e: e.tensor_tensor(out=xres[:, i, :], in0=xres[:, i, :], in1=gb[:, 0, :], op=ALU.mult),
                [xk, "gb"], [xk])
            vop(lambda e: e.tensor_tensor(out=xres[:, i, :], in0=xres[:, i, :], in1=gb[:, 1, :], op=ALU.add),
                [xk, "gb"], [xk])

        def load_gb(l, j):
            dma("sp", gb[:, 0, :], ln_g[l, j, :].partition_broadcast(128), [], ["gb"])
            dma("sp", gb[:, 1, :], ln_b[l, j, :].partition_broadcast(128), [], ["gb"])

        def resid_add(i, h, bank):
            vop(lambda e: e.scalar_tensor_tensor(out=xres[:, i, h * 512:(h + 1) * 512],
                                                 in0=xres[:, i, h * 512:(h + 1) * 512], scalar=float(ALPHA),
                                                 in1=psum[bank][:, :], op0=ALU.mult, op1=ALU.add),
                [("x", i), PK(bank)], [("x", i)])

        evac_flip = [0]
        pending_ln = []

        def flush_ln():
            while pending_ln:
                ti = pending_ln.pop(0)
                ln_tile(ti, ti % 2)

        def evac_copy(out, in_, reads, writes, scale=None):
            evac_flip[0] ^= 1
            if evac_flip[0]:
                if scale is None:
                    return act(out, in_, AF.Copy, reads, writes)
                return act(out, in_, AF.Copy, reads, writes, scale=float(scale))
            if scale is None:
                return vop(lambda e: e.tensor_copy(out=out, in_=in_), reads, writes)
            return vop(lambda e: e.tensor_scalar_mul(out=out, in0=in_, scalar1=float(scale)), reads, writes)

        def dve_copy(out, in_, reads, writes, scale=None):
            if scale is None:
                return vop(lambda e: e.tensor_copy(out=out, in_=in_), reads, writes)
            return vop(lambda e: e.tensor_scalar_mul(out=out, in0=in_, scalar1=float(scale)), reads, writes)

        def ffn_phase(l, j, ln_idx):
            xTg = [carve(k * 8192, 8192, BF16, "p (c t) -> p c t", c=8) for k in range(2)]
            aT = carve(16384, 22528, BF16, "p (f t) -> p f t", f=NFC)
            sil = [carve(38912 + k * 1024, 1024, F32) for k in range(2)]
            NGS = 3
            wgu = [carve(40960 + k * 4096, 4096, BF16, "p (a c f) -> p a c f", a=2, c=8) for k in range(NGS)]
            NDS = 3
            wdb = [carve(53248 + k * 2048, 2048, BF16, "p (f d) -> p f d", f=4) for k in range(NDS)]
            assert 53248 + NDS * 2048 <= ARENA
            for fam, lo, n in (("xTg", 0, 16384), ("aT", 16384, 22528), ("sil", 38912, 2048), ("wgu", 40960, 12288),
                               ("wd", 53248, 6144)):
                sc.reg(fam, lo, n)
            wg_v = wg_d[l, j].rearrange("(c p) f -> p c f", p=128)
            wu_v = wu_d[l, j].rearrange("(c p) f -> p c f", p=128)
            wd_v = wd_d[l, j].rearrange("(f p) d -> p f d", p=128)
            NG = 2
            gu_items = [(g, u) for g in range(NG) for u in range(11)]
            d_units = [(0, 4), (4, 4), (8, 4), (12, 4), (16, 4), (20, 2)]
            d_items = [(g, q, h, k) for g in range(NG) for q in range(2) for h in range(2) for k in range(6)]
            gu_next = [0]
            d_next = [0]

            def gu_load():
                n = gu_next[0]
                if n >= len(gu_items):
                    return
                gu_next[0] += 1
                _, u = gu_items[n]
                s = n % NGS
                dma("pool", wgu[s][:, 0], wg_v[:, :, u * 256:(u + 1) * 256], [], [("wgu", s)])
                dma("pool", wgu[s][:, 1], wu_v[:, :, u * 256:(u + 1) * 256], [], [("wgu", s)])

            def d_load():
                n = d_next[0]
                if n >= len(d_items):
                    return
                d_next[0] += 1
                _, _, h, k = d_items[n]
                f0, nf = d_units[k]
                s = n % NDS
                dma("pool", wdb[s][:, 0:nf, :], wd_v[:, f0:f0 + nf, h * 512:(h + 1) * 512], [], [("wd", s)])
            for _ in range(NGS):
                gu_load()
            for _ in range(NDS):
                d_load()
            gu_i = 0
            d_i = 0

            def one_transpose_group(g, c, th):
                xt = xTg[g % 2]
                xtk = ("xTg", g % 2)
                bank = (2 * c + th) % 4
                for tl in range(4):
                    i = 8 * g + 4 * th + tl
                    tr(psum[bank][:, tl * 128:(tl + 1) * 128], xres[:, i, c * 128:(c + 1) * 128],
                       [("x", i), "ident"], [PK(bank)], signal=(tl == 3))
                act(xt[:, c, th * 512:(th + 1) * 512], psum[bank][:, :], AF.Copy, [PK(bank)], [(xtk, c, th)])

            def group_transposes(g):
                for c in range(8):
                    for th in range(2):
                        one_transpose_group(g, c, th)
            group_transposes(0)
            flush_ln()
            load_gb(l, ln_idx)
            for g in range(NG):
                xt = xTg[g % 2]
                xtk = ("xTg", g % 2)
                for u in range(11):
                    s = gu_i % NGS
                    gu_i += 1
                    for fl in range(2):
                        fc = 2 * u + fl
                        for th in range(2):
                            bg = th + 4 * (fc % 2)
                            bu = 2 + th + 4 * (fc % 2)
                            for c in range(8):
                                mm(psum[bg][:, :], wgu[s][:, 0, c, fl * 128:(fl + 1) * 128], xt[:, c, th * 512:(th + 1) * 512],
                                   c == 0, c == 7, [("wgu", s), (xtk, c, th)], [PK(bg)], signal=(c == 7))
                            for c in range(8):
                                mm(psum[bu][:, :], wgu[s][:, 1, c, fl * 128:(fl + 1) * 128], xt[:, c, th * 512:(th + 1) * 512],
                                   c == 0, c == 7, [("wgu", s), (xtk, c, th)], [PK(bu)], signal=(c == 7))
                            act(sil[th][:, :], psum[bg][:, :], AF.Silu, [PK(bg)], [("sil", th)])
                            vop(lambda e, fc=fc, bu=bu, th=th: e.scalar_tensor_tensor(
                                out=aT[:, fc, th * 512:(th + 1) * 512], in0=sil[th][:, :], scalar=0.5, in1=psum[bu][:, :],
                                op0=ALU.mult, op1=ALU.mult), [("sil", th), PK(bu)], [("aT", fc, th)])
                    gu_load()
                    if pending_ln:
                        ti = pending_ln.pop(0)
                        ln_tile(ti, ti % 2)
                for q in range(2):
                    for h in range(2):
                        bb = 4 if (2 * q + h) % 2 == 0 else 0
                        hoist = (q == 1 and h == 0 and g + 1 < NG)
                        tgroups = [(c, th) for c in range(8) for th in range(2)] if hoist else []
                        for k in range(6):
                            f0, nf = d_units[k]
                            s = d_i % NDS
                            d_i += 1
                            for fl in range(nf):
                                fc = f0 + fl
                                for tl in range(4):
                                    t0 = (4 * q + tl) * 128
                                    mm(psum[bb + tl][:, :], aT[:, fc, t0:t0 + 128], wdb[s][:, fl, :],
                                       fc == 0, fc == NFC - 1, [("wd", s), ("aT", fc, q)], [PK(bb + tl)],
                                       signal=(fc == NFC - 1 or (fl == nf - 1 and tl == 3)))
                            d_load()
                            for _ in range(3):
                                if tgroups:
                                    c, th = tgroups.pop(0)
                                    one_transpose_group(g + 1, c, th)
                        for tl in range(4):
                            resid_add(8 * g + 4 * q + tl, h, bb + tl)
                        for _ in range(2):
                            if pending_ln:
                                ti = pending_ln.pop(0)
                                ln_tile(ti, ti % 2)
                    pending_ln.extend(8 * g + 4 * q + tl for tl in range(4))

        def full_transpose(xT):
            for tb in range(4):
                if tb == 3:
                    flush_ln()
                for c in range(8):
                    bank = (c + 4 * tb) % 4
                    for tl in range(4):
                        i = 4 * tb + tl
                        tr(psum[bank][:, tl * 128:(tl + 1) * 128], xres[:, i, c * 128:(c + 1) * 128],
                           [("x", i), "ident"], [PK(bank)], signal=(tl == 3))
                    act(xT[:, c, tb * 512:(tb + 1) * 512], psum[bank][:, :], AF.Copy, [PK(bank)], [("xT", c, tb)])

        sbank = [0]
        obank = [0]
        ptslot = [0]

        DUMMY_N = 384
        deferred_norm = []

        class Filler:
            def __init__(self, items, pace):
                self.items = list(items)
                self.pace = pace
                self.n = 0

            def __call__(self):
                self.n += 1
                if self.items and self.n % self.pace == 0:
                    self.items.pop(0)()
                    return True
                return False

            def flush(self):
                while self.items:
                    self.items.pop(0)()

        def flush_norm():
            while deferred_norm:
                deferred_norm.pop(0)()

        def attn_core(qt, qk, q0, kt, kk, k0, K, vfn, vk, nkb_fn, causal, out_ap_fn, out_key_fn, PT, rec, filler=None, tok=None):
            identb = ident[:, :].bitcast(BF16)
            for tb in range(4):
                nkb = nkb_fn(tb)
                ob = 3 + obank[0]
                obank[0] ^= 1
                pend = []

                def issue_s(kb):
                    jj = kb - 4 * tb if causal else -1
                    c0 = max(0, jj) * 128
                    b = sbank[0]
                    sbank[0] = (sbank[0] + 1) % 3
                    mm(psum[b][:, c0:512], kt[k0:k0 + K, kb * 128:(kb + 1) * 128],
                       qt[q0:q0 + K, tb * 512 + c0:(tb + 1) * 512], True, True,
                       [kk, qk], [PK(b)], signal=True)
                    did = False
                    if filler is not None and c0 == 0:
                        did = filler()
                    if DUMMY_N and c0 == 0 and not did:
                        n1 = min(DUMMY_N, 256)
                        mm(psum[7][:, 0:n1], tri[:, :], identb[:, 0:n1], True, True, ["tri", "ident"], [PK(7)], signal=False)
                        if DUMMY_N > 256:
                            mm(psum[7][:, 256:DUMMY_N], tri[:, :], identb[:, 0:DUMMY_N - 256], True, True,
                               ["tri", "ident"], [PK(7)], signal=False)
                    sl = ptslot[0]
                    ptslot[0] = (ptslot[0] + 1) % 3
                    act(PT[sl][:, c0:512], psum[b][:, c0:512], AF.Exp, [PK(b)], [("PT", sl)])
                    if jj >= 0:
                        vop(lambda e: e.tensor_tensor(out=PT[sl][:, c0:c0 + 128], in0=PT[sl][:, c0:c0 + 128],
                                                      in1=tri[:, :], op=ALU.mult), [("PT", sl), "tri"], [("PT", sl)],
                            eng="pool")
                    return (kb, sl, c0)

                def issue_pv(item):
                    kb, sl, c0 = item
                    if tok is None:
                        mm(psum[ob][:, c0:512], vfn(kb), PT[sl][:, c0:512], kb == 0, kb == nkb - 1,
                           [vk, ("PT", sl)], [PK(ob)], signal=True)
                        return
                    for sub in range(c0 // 128, 4):
                        mm(psum[ob][:, sub * 65:(sub + 1) * 65], PT[sl][:, sub * 128:(sub + 1) * 128], vfn(kb)[:, 0:65],
                           kb == 0 and sub == 0, kb == nkb - 1 and sub == 3, [vk, ("PT", sl)], [PK(ob)],
                           signal=(sub == 3), sgc=True)
                for kb in range(nkb):
                    pend.append(issue_s(kb))
                    if kb == 1:
                        flush_norm()
                    if len(pend) > 2:
                        issue_pv(pend.pop(0))
                while pend:
                    issue_pv(pend.pop(0))
                oa = out_ap_fn(tb)
                ok = out_key_fn(tb)

                def norm(ob=ob, oa=oa, ok=ok, tb=tb):
                    if tok is None:
                        vop(lambda e: e.reciprocal(out=rec[64:128, :], in_=psum[ob][64:128, :]), [PK(ob)], ["rec"])
                        vop(lambda e: e.tensor_tensor(out=oa, in0=psum[ob][0:64, :], in1=rec[64:128, :], op=ALU.mult),
                            [PK(ob), "rec"], [ok])
                        return
                    mixtok, col0 = tok
                    par = ob % 2
                    ov = psum[ob][:, 0:260].rearrange("p (s e) -> p s e", e=65)
                    vop(lambda e: e.reciprocal(out=rden[:, par, :], in_=ov[:, :, 64]), [PK(ob)], [("rden", par)])
                    for sub in range(4):
                        vop(lambda e, sub=sub: e.tensor_scalar_mul(out=mixtok[:, 4 * tb + sub, col0:col0 + 64],
                                                                    in0=ov[:, sub, 0:64], scalar1=rden[:, par, sub:sub + 1]),
                            [PK(ob), ("rden", par)], [("mixtok", 4 * tb + sub)])
                deferred_norm.append(norm)
            flush_norm()

        def proj_pair(w, wkeys, xT, dst, scale=None, M=128):
            for tb in range(4):
                b = 5 + (tb % 2)
                for c in range(8):
                    mm(psum[b][0:M, :], w[:, c, 0:M], xT[:, c, tb * 512:(tb + 1) * 512], c == 0, c == 7,
                       list(wkeys) + [("xT", c, tb)], [PK(b)], signal=(c == 7))
                for (oap, r0, r1, key) in dst(tb):
                    dve_copy(oap, psum[b][r0:r1, :], [PK(b)], [key], scale=scale)

        def mem_attention(l, xT, mixT, mix_c0, w_in_v, memq_c0, A, mixtok=None):
            wA, wB, kmT, Vm, PT, rec = A["wq"], A["wk"], A["kmT"], A["Vm"], A["PT"], A["rec"]
            qm, qmk = A["qm"], A["qmk"]
            wkv_v = wkv_d[l].rearrange("(c p) f -> p c f", p=128)
            vop(lambda e: e.memset(Vm[:, :, :, 64:128], 1.0), [], ["Vm"], eng="pool")
            for hp in range(2):
                dma("pool", wA[:, :, :], wkv_v[:, :, hp * 128:(hp + 1) * 128], [], ["wq"])
                for c in range(8):
                    mm(psum[7][:, 0:256], wA[:, c, :], memT[:, c, :], c == 0, c == 7, ["wq", "memT"], [PK(7)], signal=(c == 7))
                dve_copy(kmT[:, hp, :], psum[7][:, 0:256], [PK(7)], [("kmT", hp)])
            for vh in range(2):
                dma("pool", wB[:, :, :], wkv_v[:, :, 256 + vh * 128:256 + (vh + 1) * 128], [], ["wk"])
                for mb in range(2):
                    for c in range(8):
                        mm(psum[7][:, 0:128], memT[:, c, mb * 128:(mb + 1) * 128], wB[:, c, :], c == 0, c == 7,
                           ["wk", "memT"], [PK(7)], signal=(c == 7))
                    dve_copy(Vm[:, mb, 2 * vh:2 * vh + 2, 0:64],
                             psum[7][:, 0:128].rearrange("p (h e) -> p h e", h=2), [PK(7)], ["Vm"])
            for hp in range(2):
                dma("pool", wA[:, :, :], w_in_v[:, :, memq_c0 + hp * 128:memq_c0 + (hp + 1) * 128], [], ["wq"])
                proj_pair(wA, ["wq"], xT, lambda tb: [(qm[:, tb * 512:(tb + 1) * 512], 0, 128, qmk)], scale=0.125)
                for s2 in range(2):
                    h = 2 * hp + s2
                    attn_core(qm, qmk, s2 * 64, kmT[:, hp, :], ("kmT", hp), s2 * 64, 64,
                              lambda kb, h=h: Vm[:, kb, h, :], "Vm", lambda tb: 2, False,
                              lambda tb, h=h: mixT[(h % 2) * 64:(h % 2) * 64 + 64, mix_c0 + h // 2, tb * 512:(tb + 1) * 512],
                              lambda tb, h=h: ("mixT", mix_c0 + h // 2, tb), PT, rec,
                              tok=None if mixtok is None else (mixtok, 768 + 64 * h))

        def wo_phase(l, mix_chunks, wo_loader, wo_keys, tokinfo=None):
            flush_ln()
            load_gb(l, 1)
            wo_loader()
            n = len(mix_chunks)

            def prep_tile(i):
                mixtok, mixTt = tokinfo
                tbk = i % 2
                pb = psum[tbk][:, :].bitcast(BF16)
                for c in range(8):
                    tr16(pb[:, c * 128:(c + 1) * 128], mixtok[:, i, c * 128:(c + 1) * 128],
                         [("mixtok", i), "identb16"], [PK(tbk)], signal=(c == 7))
                act(mixTt[i % 2][:, :], pb[:, :], AF.Copy, [PK(tbk)], [("mixTt", i % 2)])
            if tokinfo is not None:
                prep_tile(0)
            for i in range(16):
                if tokinfo is not None and i + 1 < 16:
                    prep_tile(i + 1)
                for h in range(2):
                    b = 4 + 2 * (i % 2) + h
                    for ci, (lf, rf, kf) in enumerate(mix_chunks):
                        mm(psum[b][:, :], lf(i), rf(h), ci == 0, ci == n - 1, list(kf(i)) + list(wo_keys), [PK(b)],
                           signal=(ci == n - 1))
                    resid_add(i, h, b)
                ln_tile(i, i % 2)

        def fox_phase(l, jx):
            xT = carve(0, 16384, BF16, "p (c t) -> p c t", c=8)
            mixT = carve(16384, 16384, BF16, "p (c t) -> p c t", c=8)
            mixtok = carve(16384, 16384, BF16, "p (i c) -> p i c", i=16)
            ptmp = carve(54400, 2048, BF16)
            A = {}
            A["wq"] = carve(32768, 1024, BF16, "p (c f) -> p c f", c=8)
            A["wk"] = carve(33792, 1024, BF16, "p (c f) -> p c f", c=8)
            wv = carve(34816, 1024, BF16, "p (c f) -> p c f", c=8)
            wf = carve(35840, 128, BF16, "p (c f) -> p c f", c=8)
            qa = [[carve(35968 + (2 * p + k) * 2048, 2048, BF16) for k in range(2)] for p in range(2)]
            ka = [[carve(44160 + (2 * p + k) * 2048, 2048, BF16) for k in range(2)] for p in range(2)]
            Vaug = [carve(52352 + p * 4096, 4096, BF16, "p (i s e) -> p i s e", i=16, s=2) for p in range(2)]
            A["PT"] = [carve(60544 + k * 512, 512, BF16) for k in range(3)]
            A["rec"] = carve(62080, 1024, F32)
            pc2 = carve(63104, 2048, BF16)
            dec = carve(56448, 4096, F32)
            A["kmT"] = carve(52352, 512, BF16, "p (h m) -> p h m", h=2)
            A["Vm"] = carve(52864, 1024, BF16, "p (b h e) -> p b h e", b=2, h=4)
            A["qm"], A["qmk"] = qa[0][0], ("qa", 0, 0)
            assert 65152 <= ARENA
            for fam, lo, n in (("xT", 0, 16384), ("mixT", 16384, 16384), ("wq", 32768, 1024), ("wk", 33792, 1024),
                               ("wv", 34816, 1024), ("wf", 35840, 128), ("qa", 35968, 8192), ("ka", 44160, 8192),
                               ("Vaug", 52352, 8192), ("PT", 60544, 1536), ("rec", 62080, 1024), ("dec", 56448, 4096),
                               ("pc", 63104, 2048), ("mixtok", 16384, 16384), ("mixTt", 52352, 2048), ("kmT", 52352, 512), ("Vm", 52864, 1024), ("ptmp", 54400, 2048)):
                sc.reg(fam, lo, n)
            w_in_v = fox_w[jx].rearrange("(c p) f -> p c f", p=128)
            full_transpose(xT)
            mem_attention(l, xT, mixT, 6, w_in_v, 2316, A, mixtok=mixtok)
            wq, wk = A["wq"], A["wk"]
            negb = smallc[0:12, 0:1]
            dma("sp", smallc[0:12, 1:2], fox_b[jx, :].rearrange("(h o) -> h o", o=1), [], ["bf"])
            vop(lambda e: e.tensor_scalar_mul(out=negb, in0=smallc[0:12, 1:2], scalar1=-1.0), ["bf"], ["negb"])
            dma("pool", wf[:, :, 0:12], w_in_v[:, :, 2304:2316], [], ["wf"])
            for tb in range(4):
                for c in range(8):
                    mm(psum[7][0:12, :], wf[:, c, 0:12], xT[:, c, tb * 512:(tb + 1) * 512], c == 0, c == 7,
                       ["wf", ("xT", c, tb)], [PK(7)], signal=(c == 7))
                act(dec[0:12, tb * 512:(tb + 1) * 512], psum[7][0:12, :], AF.Exp, [PK(7), "negb"], ["dec"],
                    bias=negb, scale=-1.0)
            act(dec[0:12, :], dec[0:12, :], AF.Ln, ["dec"], ["dec"], bias=1.0)
            vop(lambda e: e.memset(ptmp[0:12, :], 1.0), [], ["ptmp"])
            vop(lambda e: e.tensor_tensor_scan(out=dec[0:12, :], data0=ptmp[0:12, :], data1=dec[0:12, :],
                                               initial=0.0, op0=ALU.mult, op1=ALU.add), ["dec", "ptmp"], ["dec"])
            for r in range(3):
                vop(lambda e, r=r: e.tensor_copy(out=pc2[32 * r:32 * r + 12, :], in_=dec[0:12, :]), ["dec"], ["pc"])
                if r < 2:
                    vop(lambda e: e.tensor_copy(out=ptmp[0:12, :], in_=dec[0:12, :]), ["dec"], ["ptmp"])
                    vop(lambda e: e.tensor_tensor(out=dec[0:12, :], in0=dec[0:12, :], in1=ptmp[0:12, :],
                                                  op=ALU.subtract), ["dec", "ptmp"], ["dec"])
            for p in range(2):
                for s2 in range(2):
                    vop(lambda e, p=p, s2=s2: e.memset(qa[p][s2][64:70, :], 1.0), [], [("qa", p, s2)])
                    vop(lambda e, p=p, s2=s2: e.memset(ka[p][s2][64:70, :], -1.0), [], [("ka", p, s2)])
                vop(lambda e, p=p: e.memset(Vaug[p][:, :, :, 64:128], 1.0), [], [("Vaug", p)], eng="pool")

            def prep_items(hp):
                p = hp % 2
                items = []

                def loads():
                    dma("pool", wq[:, :, :], w_in_v[:, :, hp * 128:(hp + 1) * 128], [], ["wq"])
                    dma("pool", wk[:, :, :], w_in_v[:, :, 768 + hp * 128:768 + (hp + 1) * 128], [], ["wk"])
                    dma("pool", wv[:, :, :], w_in_v[:, :, 1536 + hp * 128:1536 + (hp + 1) * 128], [], ["wv"])
                    for s2 in range(2):
                        h = 2 * hp + s2
                        for r in range(3):
                            dma("sp", qa[p][s2][64 + r:65 + r, :], pc2[32 * r + h:32 * r + h + 1, :], ["pc"], [("qa", p, s2)])
                            dma("sp", ka[p][s2][67 + r:68 + r, :], pc2[32 * r + h:32 * r + h + 1, :], ["pc"], [("ka", p, s2)])
                items.append(loads)

                def proj_item(w, wkey, dst, dkey, scale, tb):
                    def f():
                        b = 5 + (tb % 2)
                        for c in range(8):
                            mm(psum[b][:, :], w[:, c, :], xT[:, c, tb * 512:(tb + 1) * 512], c == 0, c == 7,
                               [wkey, ("xT", c, tb)], [PK(b)], signal=(c == 7))
                        for s2 in range(2):
                            dve_copy(dst[p][s2][0:64, tb * 512:(tb + 1) * 512], psum[b][s2 * 64:(s2 + 1) * 64, :], [PK(b)],
                                     [(dkey, p, s2)], scale=scale)
                    return f
                for tb in range(4):
                    items.append(proj_item(wq, "wq", qa, "qa", 0.125, tb))
                for tb in range(4):
                    items.append(proj_item(wk, "wk", ka, "ka", None, tb))

                def v_item(q4):
                    def f():
                        b = 5 + (q4 % 2)
                        for tl in range(4):
                            i = 4 * q4 + tl
                            for c in range(8):
                                mm(psum[b][:, tl * 128:(tl + 1) * 128], xT[:, c, i * 128:(i + 1) * 128], wv[:, c, :],
                                   c == 0, c == 7, ["wv", ("xT", c, q4)], [PK(b)], signal=(c == 7 and tl == 3))
                        dve_copy(Vaug[p][:, 4 * q4:4 * q4 + 4, :, 0:64],
                                 psum[b][:, :].rearrange("p (i s e) -> p i s e", i=4, s=2), [PK(b)], [("Vaug", p)])
                    return f
                for q4 in range(4):
                    items.append(v_item(q4))
                return items
            for it in prep_items(0):
                it()
            for hp in range(6):
                p = hp % 2
                fill = Filler(prep_items(hp + 1), 5) if hp + 1 < 6 else None
                for s2 in range(2):
                    h = 2 * hp + s2
                    attn_core(qa[p][s2], ("qa", p, s2), 0, ka[p][s2], ("ka", p, s2), 0, 70,
                              lambda kb, p=p, s2=s2: Vaug[p][:, kb, s2, :], ("Vaug", p), lambda tb: 4 * tb + 4, True,
                              lambda tb, h=h: mixT[(h % 2) * 64:(h % 2) * 64 + 64, h // 2, tb * 512:(tb + 1) * 512],
                              lambda tb, h=h: ("mixT", h // 2, tb), A["PT"], A["rec"], filler=fill, tok=(mixtok, 64 * h))
                if fill is not None:
                    fill.flush()
            wo_sb = carve(35968, 8192, BF16, "p (c d) -> p c d", c=8)
            wo_keys = [("qa", 0, 0), ("qa", 0, 1), ("qa", 1, 0), ("qa", 1, 1)]

            def wo_loader():
                wo_v = wo_d[l].rearrange("(c p) d -> p c d", p=128)
                for hh in range(2):
                    dma("pool", wo_sb[:, 4 * hh:4 * hh + 4, :], wo_v[:, 4 * hh:4 * hh + 4, :], [], wo_keys)
            mixTt = [carve(52352 + k * 1024, 1024, BF16) for k in range(2)]
            chunks = []
            for c in range(8):
                chunks.append((lambda i, c=c: mixTt[i % 2][:, c * 128:(c + 1) * 128],
                               lambda h, c=c: wo_sb[:, c, h * 512:(h + 1) * 512],
                               lambda i, c=c: [("mixTt", i % 2)]))
            wo_phase(l, chunks, wo_loader, wo_keys, tokinfo=(mixtok, mixTt))

        def pool_phase(l, jx):
            xT = carve(0, 16384, BF16, "p (c t) -> p c t", c=8)
            mixT = carve(16384, 20480, BF16, "p (c t) -> p c t", c=10)
            o = 36864
            A = {}
            A["wq"] = carve(o, 1024, BF16, "p (c f) -> p c f", c=8)
            wu = carve(o, 1536, BF16, "p (c f) -> p c f", c=8); o += 1024
            A["wk"] = carve(o, 1024, BF16, "p (c f) -> p c f", c=8); o += 1024
            qa = [carve(o + k * 2048, 2048, BF16) for k in range(2)]; o += 4096
            A["qm"], A["qmk"] = qa[0], ("qa", 0)
            A["PT"] = [carve(o + k * 512, 512, BF16) for k in range(3)]; o += 1536
            A["rec"] = carve(o, 1024, F32); o += 1024
            A["kmT"] = carve(o, 512, BF16, "p (h m) -> p h m", h=2); o += 512
            A["Vm"] = carve(o, 1024, BF16, "p (b h e) -> p b h e", b=2, h=4); o += 1024
            assert o == 47104
            pooled = carve(o, 4096, BF16, "p (k t) -> p k t", k=2)
            uu = carve(o + 4096, 4096, F32)
            Sab = [carve(o + 8192, 4096, F32), carve(o + 12288, 4096, F32)]
            assert o + 16384 <= ARENA
            for fam, lo, n in (("xT", 0, 16384), ("mixT", 16384, 20480), ("wq", 36864, 1024), ("wk", 37888, 1024),
                               ("qa", 38912, 4096), ("PT", 43008, 1536), ("rec", 44544, 1024), ("kmT", 45568, 512),
                               ("Vm", 46080, 1024), ("pooled", 47104, 4096), ("uu", 51200, 4096), ("S", 55296, 8192)):
                sc.reg(fam, lo, n)
            w_in_v = pool_w[jx].rearrange("(c p) f -> p c f", p=128)
            full_transpose(xT)
            mem_attention(l, xT, mixT, 8, w_in_v, 768, A)
            for ch in range(8):
                dma("sp", smallc[0:96, 8 + ch:9 + ch], pool_scale[jx, ch * 96:(ch + 1) * 96].rearrange("(p o) -> p o", o=1),
                    [], ["pscale"])
            for g in range(4):
                w = 2 ** (g + 1)
                dma("pool", wu[:, :, :], w_in_v[:, :, g * 192:(g + 1) * 192], [], ["wq", "wk"])
                dma("pool", wgrp_sb[0:96, :, :], pool_grp[jx, g].rearrange("(k p) o -> p k o", p=96), [], ["wgrp"])
                for kc in range(2):
                    for tb in range(4):
                        b = 5 + (tb % 2)
                        for c in range(8):
                            mm(psum[b][0:96, :], wu[:, c, kc * 96:(kc + 1) * 96], xT[:, c, tb * 512:(tb + 1) * 512],
                               c == 0, c == 7, ["wq", "wk", ("xT", c, tb)], [PK(b)], signal=(c == 7))
                        dve_copy(uu[0:96, tb * 512:(tb + 1) * 512], psum[b][0:96, :], [PK(b)], ["uu"])
                    src, srck = uu, "uu"
                    for k in range(g + 1):
                        sh = 2 ** k
                        dst, dstk = Sab[k % 2], ("S", k % 2)
                        vop(lambda e, src=src, dst=dst, sh=sh: e.tensor_tensor(
                            out=dst[0:96, sh:S], in0=src[0:96, sh:S], in1=src[0:96, 0:S - sh], op=ALU.add),
                            [srck], [dstk])
                        vop(lambda e, src=src, dst=dst, sh=sh: e.tensor_copy(out=dst[0:96, 0:sh], in_=src[0:96, 0:sh]),
                            [srck], [dstk])
                        src, srck = dst, dstk
                    oth, othk = Sab[(g + 1) % 2], ("S", (g + 1) % 2)
                    vop(lambda e, src=src, kc=kc, w=w: e.scalar_tensor_tensor(
                        out=pooled[0:96, kc, :], in0=src[0:96, :], scalar=1.0 / w, in1=uu[0:96, :],
                        op0=ALU.mult, op1=ALU.subtract), [srck, "uu"], [("pooled", kc)])
                    vop(lambda e, src=src, oth=oth, w=w: e.tensor_tensor(
                        out=oth[0:96, 0:w - 1], in0=src[0:96, 0:w - 1], in1=smallc[0:96, 16:16 + w - 1], op=ALU.mult),
                        [srck, "rinv"], [othk])
                    vop(lambda e, oth=oth, kc=kc, w=w: e.tensor_tensor(
                        out=pooled[0:96, kc, 0:w - 1], in0=oth[0:96, 0:w - 1], in1=uu[0:96, 0:w - 1], op=ALU.subtract),
                        [othk, "uu"], [("pooled", kc)])
                for oc in range(2):
                    for tb in range(4):
                        b = 5 + (tb % 2)
                        for kc in range(2):
                            mm(psum[b][0:96, :], wgrp_sb[0:96, kc, oc * 96:(oc + 1) * 96],
                               pooled[0:96, kc, tb * 512:(tb + 1) * 512], kc == 0, kc == 1,
                               ["wgrp", ("pooled", kc)], [PK(b)], signal=(kc == 1))
                        ch = 2 * g + oc
                        vop(lambda e, b=b, ch=ch, tb=tb: e.tensor_scalar_mul(
                            out=mixT[0:96, ch, tb * 512:(tb + 1) * 512], in0=psum[b][0:96, :],
                            scalar1=smallc[0:96, 8 + ch:9 + ch]), [PK(b), "pscale"], [("mixT", ch, tb)])
            wo_p = carve(47104, 8192, BF16, "p (c d) -> p c d", c=8)
            wo_m = carve(47104 + 8192, 2048, BF16, "p (c d) -> p c d", c=2)
            wo_keys = [("pooled", 0), ("pooled", 1), "uu", ("S", 0), ("S", 1)]

            def wo_loader():
                dma("pool", wo_p[0:96, :, :], wo_d[l][0:768, :].rearrange("(c p) d -> p c d", p=96), [], wo_keys)
                dma("pool", wo_m[:, :, :], wo_d[l][768:1024, :].rearrange("(c p) d -> p c d", p=128), [], wo_keys)
            chunks = []
            for c in range(8):
                chunks.append((lambda i, c=c: mixT[0:96, c, i * 128:(i + 1) * 128],
                               lambda h, c=c: wo_p[0:96, c, h * 512:(h + 1) * 512],
                               lambda i, c=c: [("mixT", c, i // 4)]))
            for c in range(2):
                chunks.append((lambda i, c=c: mixT[:, 8 + c, i * 128:(i + 1) * 128],
                               lambda h, c=c: wo_m[:, c, h * 512:(h + 1) * 512],
                               lambda i, c=c: [("mixT", 8 + c, i // 4)]))
            wo_phase(l, chunks, wo_loader, wo_keys)

        def mla_phase(l, jx):
            xT = carve(0, 16384, BF16, "p (c t) -> p c t", c=8)
            mixT = carve(16384, 16384, BF16, "p (c t) -> p c t", c=8)
            mixtok = carve(16384, 16384, BF16, "p (i c) -> p i c", i=16)
            o = 32768
            A = {}
            wreg = o
            A["wq"] = carve(o, 1024, BF16, "p (c f) -> p c f", c=8); o += 1024
            A["wk"] = carve(o, 1024, BF16, "p (c f) -> p c f", c=8); o += 1024
            qa = [carve(o + k * 2048, 2048, BF16) for k in range(2)]; o += 4096
            ka = [carve(o + k * 2048, 2048, BF16) for k in range(2)]; o += 4096
            A["qm"], A["qmk"] = qa[0], ("qa", 0)
            A["PT"] = [carve(o + k * 512, 512, BF16) for k in range(3)]; o += 1536
            A["rec"] = carve(o, 1024, F32); o += 1024
            A["kmT"] = carve(o, 512, BF16, "p (h m) -> p h m", h=2); o += 512
            A["Vm"] = carve(o, 1024, BF16, "p (b h e) -> p b h e", b=2, h=4); o += 1024
            assert o == 47104
            cqT = carve(o, 6144, BF16, "p (c t) -> p c t", c=3)
            angf = carve(o, 4096, F32)
            o += 6144
            ckvT = carve(o, 4096, BF16, "p (c t) -> p c t", c=2); o += 4096
            Vaug = carve(o, 4096, BF16, "p (i s e) -> p i s e", i=16, s=2)
            posi = carve(o, 4096, F32).bitcast(I32)
            posf = carve(o, 4096, F32)
            o += 4096
            t1 = carve(o, 1024, F32); t2 = carve(o + 1024, 1024, F32); o += 2048
            sqb = carve(o - 1024, 512, BF16)
            assert o <= ARENA
            Ct, Sn = qa[0], ka[0]
            for fam, lo, n in (("xT", 0, 16384), ("mixT", 16384, 16384), ("wq", 32768, 1024), ("wk", 33792, 1024),
                               ("qa", 34816, 4096), ("ka", 38912, 4096), ("CtT", 34816, 2048), ("SnT", 38912, 2048),
                               ("PT", 43008, 1536), ("rec", 44544, 1024), ("kmT", 45568, 512), ("Vm", 46080, 1024), ("cq", 47104, 6144), ("ang", 47104, 4096),
                               ("ckv", 53248, 4096), ("kf", 53248, 4096), ("Vaug", 57344, 4096), ("mixtok", 16384, 16384), ("mixTt", 57344, 2048), ("t1", 61440, 1024),
                               ("t2", 62464, 1024), ("sqb", 62464, 512)):
                sc.reg(fam, lo, n)
            w_in_v = mla_w[jx].rearrange("(c p) f -> p c f", p=128)
            full_transpose(xT)
            mem_attention(l, xT, mixT, 6, w_in_v, 672, A, mixtok=mixtok)
            wq, wk = A["wq"], A["wk"]
            invf = smallc[64:96, 32:33]
            vop(lambda e: e.iota(smallc[64:96, 33:34], pattern=[[0, 1]], base=0, channel_multiplier=1,
                                 allow_small_or_imprecise_dtypes=True), [], ["invf"], eng="pool")
            vop(lambda e: e.tensor_single_scalar(out=smallc[64:96, 34:35], in_=smallc[64:96, 33:34], scalar=16.0,
                                                 op=ALU.is_ge), ["invf"], ["invf2"])
            vop(lambda e: e.scalar_tensor_tensor(out=smallc[64:96, 33:34], in0=smallc[64:96, 34:35], scalar=-16.0,
                                                 in1=smallc[64:96, 33:34], op0=ALU.mult, op1=ALU.add),
                ["invf", "invf2"], ["invf"])
            act(invf, smallc[64:96, 33:34], AF.Exp, ["invf"], ["invf"], scale=-math.log(10000.0) / 16.0)
            dma("sp", posi[64:96, :], pos_d[0, :].partition_broadcast(32), [], ["Vaug"])
            vop(lambda e: e.tensor_copy(out=posf[64:96, :], in_=posi[64:96, :]), ["Vaug"], ["Vaug"])
            kff = carve(47104 + 6144, 4096, F32)
            kfi = carve(47104 + 6144, 4096, F32).bitcast(I32)
            TWO_PI = float(2 * math.pi)
            for (tab, tk, shift) in ((Sn, "SnT", 0.0), (Ct, "CtT", 0.5 * math.pi)):
                vop(lambda e, shift=shift: e.tensor_scalar(out=angf[64:96, :], in0=posf[64:96, :], scalar1=invf,
                                                           scalar2=float(shift), op0=ALU.mult, op1=ALU.add),
                    ["Vaug", "invf"], ["ang"])
                vop(lambda e: e.tensor_scalar(out=kfi[64:96, :], in0=angf[64:96, :], scalar1=float(1.0 / TWO_PI),
                                              scalar2=None, op0=ALU.mult), ["ang"], ["kf"])
                vop(lambda e: e.tensor_copy(out=kff[64:96, :], in_=kfi[64:96, :]), ["kf"], ["kf"])
                vop(lambda e: e.scalar_tensor_tensor(out=angf[64:96, :], in0=kff[64:96, :], scalar=-TWO_PI,
                                                     in1=angf[64:96, :], op0=ALU.mult, op1=ALU.add), ["kf", "ang"], ["ang"])
                vop(lambda e: e.tensor_single_scalar(out=kff[64:96, :], in_=angf[64:96, :], scalar=float(math.pi),
                                                     op=ALU.is_gt), ["ang", "kf"], ["kf"])
                vop(lambda e: e.scalar_tensor_tensor(out=angf[64:96, :], in0=kff[64:96, :], scalar=-TWO_PI,
                                                     in1=angf[64:96, :], op0=ALU.mult, op1=ALU.add), ["kf", "ang"], ["ang"])
                vop(lambda e: e.tensor_scalar(out=angf[64:96, :], in0=angf[64:96, :], scalar1=float(math.pi),
                                              scalar2=float(-math.pi), op0=ALU.min, op1=ALU.max), ["ang"], ["ang"])
                act(tab[96:128, :], angf[64:96, :], AF.Sin, ["ang"], [tk])
            gq = smallc[:, 36:39]
            gkv = smallc[:, 40:42]
            for c in range(3):
                dma("sp", smallc[:, 36 + c:37 + c], mla_qn[jx, c * 128:(c + 1) * 128].rearrange("(p o) -> p o", o=1), [], ["gq"])
            for c in range(2):
                dma("sp", smallc[:, 40 + c:41 + c], mla_kvn[jx, c * 128:(c + 1) * 128].rearrange("(p o) -> p o", o=1), [], ["gq"])
            wcq = carve(wreg, 2048, BF16, "p (c f) -> p c f", c=8)
            for (c0, nch, dstT, dk, nfeat, ck) in ((0, 3, cqT, "cq", 384, 0), (384, 2, ckvT, "ckv", 256, 1)):
                for tb in range(4):
                    banks = []
                    for cc in range(nch):
                        if tb == 0 or True:
                            dma("pool", wcq[:, :, 0:128], w_in_v[:, :, c0 + cc * 128:c0 + (cc + 1) * 128], [], ["wq", "wk"])
                        b = cc
                        banks.append(b)
                        for c in range(8):
                            mm(psum[b][:, :], wcq[:, c, 0:128], xT[:, c, tb * 512:(tb + 1) * 512], c == 0, c == 7,
                               ["wq", "wk", ("xT", c, tb)], [PK(b)], signal=(c == 7))
                    for cc in range(nch):
                        act(sqb[:, :], psum[banks[cc]][:, :], AF.Square, [PK(banks[cc])], ["sqb"])
                        mm(psum[4][:, :], onesb[:, :], sqb[:, :], cc == 0, cc == nch - 1, ["onesb", "sqb"], [PK(4)], signal=True)
                    act(t1[:, :], psum[4][:, :], AF.Sqrt, [PK(4), "eps"], ["t1"], bias=epst[:, 1:2], scale=1.0 / nfeat)
                    vop(lambda e: e.reciprocal(out=t1[:, :], in_=t1[:, :]), ["t1"], ["t1"])
                    for cc in range(nch):
                        vop(lambda e, cc=cc, b=banks[cc], tb=tb, dstT=dstT: e.tensor_tensor(
                            out=dstT[:, cc, tb * 512:(tb + 1) * 512], in0=psum[b][:, :], in1=t1[:, :], op=ALU.mult),
                            [PK(banks[cc]), "t1"], [(dk, cc, tb)])
            wkr = carve(wreg, 768, BF16, "p (c f) -> p c f", c=8)
            wkr2 = carve(wreg + 768, 768, BF16, "p (c f) -> p c f", c=8)
            wraw = carve(wreg + 1536, 256, BF16, "p (c f) -> p c f", c=8)
            dma("pool", wraw[:, :, :], w_in_v[:, :, 640:672], [], ["wq", "wk"])
            vop(lambda e: e.memset(wkr[:, :, :], 0.0), ["wq", "wk"], ["wq", "wk"])
            vop(lambda e: e.memset(wkr2[:, :, :], 0.0), ["wq", "wk"], ["wq", "wk"])
            vop(lambda e: e.tensor_copy(out=wkr[:, :, 64:96], in_=wraw[:, :, :]), ["wq", "wk"], ["wq", "wk"])
            vop(lambda e: e.tensor_scalar_mul(out=wkr2[:, :, 64:80], in0=wraw[:, :, 16:32], scalar1=-1.0), ["wq", "wk"], ["wq", "wk"])
            vop(lambda e: e.tensor_copy(out=wkr2[:, :, 80:96], in_=wraw[:, :, 0:16]), ["wq", "wk"], ["wq", "wk"])

            def rope_combine(dst, dkey, bA, bB, tb):
                vop(lambda e: e.tensor_tensor(out=t1[64:96, :], in0=psum[bB][64:96, :], in1=Sn[96:128, tb * 512:(tb + 1) * 512],
                                              op=ALU.mult), [PK(bB), "SnT"], ["t1"])
                vop(lambda e: e.tensor_tensor(out=t2[64:96, :], in0=psum[bA][64:96, :], in1=Ct[96:128, tb * 512:(tb + 1) * 512],
                                              op=ALU.mult), [PK(bA), "CtT"], ["t2"])
                vop(lambda e: e.tensor_tensor(out=dst[64:96, tb * 512:(tb + 1) * 512], in0=t1[64:96, :], in1=t2[64:96, :],
                                              op=ALU.add), ["t1", "t2"], [dkey])
            for tb in range(4):
                for (w_, b) in ((wkr, 5), (wkr2, 6)):
                    for c in range(8):
                        mm(psum[b][0:96, :], w_[:, c, :], xT[:, c, tb * 512:(tb + 1) * 512], c == 0, c == 7,
                           ["wq", "wk", ("xT", c, tb)], [PK(b)], signal=(c == 7))
                rope_combine(ka[0], ("ka", 0), 5, 6, tb)
                vop(lambda e, tb=tb: e.tensor_copy(out=ka[1][64:96, tb * 512:(tb + 1) * 512],
                                                   in_=ka[0][64:96, tb * 512:(tb + 1) * 512]), [("ka", 0)], [("ka", 1)])
            stg_q = carve(wreg, 576, F32, "p (c f) -> p c f", c=3)
            stg_kv = carve(wreg + 576, 512, F32, "p (c f) -> p c f", c=2)
            wuq_h = carve(wreg + 1088, 288, BF16, "p (c f) -> p c f", c=3)
            wrot_h = carve(wreg + 1376, 288, BF16, "p (c f) -> p c f", c=3)
            wukv_h = carve(wreg + 1664, 256, BF16, "p (c f) -> p c f", c=2)
            uq_v = mla_uq[jx].rearrange("(c p) f -> p c f", p=128)
            ukv_v = mla_ukv[jx].rearrange("(c p) f -> p c f", p=128)
            WK = ["wq", "wk"]
            qscale = 96.0 ** -0.5
            vop(lambda e: e.memset(Vaug[:, :, :, 64:128], 1.0), ["Vaug"], ["Vaug", ("Vaug", 0), ("Vaug", 1)], eng="pool")

            def prep_items(h):
                s2 = h % 2
                qt, qk_, kt_, kk_ = qa[s2], ("qa", s2), ka[s2], ("ka", s2)
                items = []

                def loads():
                    dma("sp", stg_q[:, :, :], uq_v[:, :, h * 96:(h + 1) * 96], [], WK)
                    dma("sp", stg_kv[:, :, :], ukv_v[:, :, h * 128:(h + 1) * 128], [], WK)
                    for c in range(3):
                        vop(lambda e, c=c: e.tensor_scalar(out=wuq_h[:, c, :], in0=stg_q[:, c, :], scalar1=smallc[:, 36 + c:37 + c],
                                                           scalar2=float(qscale), op0=ALU.mult, op1=ALU.mult), WK + ["gq"], WK)
                    for c in range(2):
                        vop(lambda e, c=c: e.tensor_scalar_mul(out=wukv_h[:, c, :], in0=stg_kv[:, c, :],
                                                               scalar1=smallc[:, 40 + c:41 + c]), WK + ["gq"], WK)
                    vop(lambda e: e.memset(wrot_h[:, :, 0:64], 0.0), WK, WK)
                    vop(lambda e: e.tensor_scalar_mul(out=wrot_h[:, :, 64:80], in0=wuq_h[:, :, 80:96], scalar1=-1.0), WK, WK)
                    vop(lambda e: e.tensor_copy(out=wrot_h[:, :, 80:96], in_=wuq_h[:, :, 64:80]), WK, WK)
                items.append(loads)

                def qk_item(tb):
                    def f():
                        for (w_, b) in ((wuq_h, 5), (wrot_h, 6)):
                            for c in range(3):
                                mm(psum[b][0:96, :], w_[:, c, :], cqT[:, c, tb * 512:(tb + 1) * 512], c == 0, c == 2,
                                   WK + [("cq", c, tb)], [PK(b)], signal=(c == 2))
                        dve_copy(qt[0:64, tb * 512:(tb + 1) * 512], psum[5][0:64, :], [PK(5)], [qk_])
                        rope_combine(qt, qk_, 5, 6, tb)
                        for c in range(2):
                            mm(psum[5][0:64, :], wukv_h[:, c, 0:64], ckvT[:, c, tb * 512:(tb + 1) * 512], c == 0, c == 1,
                               WK + [("ckv", c, tb)], [PK(5)], signal=(c == 1))
                        dve_copy(kt_[0:64, tb * 512:(tb + 1) * 512], psum[5][0:64, :], [PK(5)], [kk_])
                    return f
                for tb in range(4):
                    items.append(qk_item(tb))

                def v_item(q8):
                    def f():
                        b = 5 + q8
                        for tl in range(8):
                            i = 8 * q8 + tl
                            for c in range(2):
                                mm(psum[b][:, tl * 64:(tl + 1) * 64], ckvT[:, c, i * 128:(i + 1) * 128], wukv_h[:, c, 64:128],
                                   c == 0, c == 1, WK + [("ckv", c, i // 4)], [PK(b)], signal=(c == 1 and tl == 7))
                        dve_copy(Vaug[:, 8 * q8:8 * q8 + 8, s2, 0:64], psum[b][:, :].rearrange("p (i e) -> p i e", i=8),
                                 [PK(b)], [("Vaug", s2)])
                    return f
                for q8 in range(2):
                    items.append(v_item(q8))
                return items
            for it in prep_items(0):
                it()
            for h in range(12):
                s2 = h % 2
                fill = Filler(prep_items(h + 1), 5) if h + 1 < 12 else None
                attn_core(qa[s2], ("qa", s2), 0, ka[s2], ("ka", s2), 0, 96,
                          lambda kb, s2=s2: Vaug[:, kb, s2, :], ("Vaug", s2), lambda tb: 4 * tb + 4, True,
                          lambda tb, h=h: mixT[(h % 2) * 64:(h % 2) * 64 + 64, h // 2, tb * 512:(tb + 1) * 512],
                          lambda tb, h=h: ("mixT", h // 2, tb), A["PT"], A["rec"], filler=fill, tok=(mixtok, 64 * h))
                if fill is not None:
                    fill.flush()
            wo_sb = carve(34816, 8192, BF16, "p (c d) -> p c d", c=8)
            wo_keys = [("qa", 0), ("qa", 1), ("ka", 0), ("ka", 1)]

            def wo_loader():
                wo_v = wo_d[l].rearrange("(c p) d -> p c d", p=128)
                for hh in range(2):
                    dma("pool", wo_sb[:, 4 * hh:4 * hh + 4, :], wo_v[:, 4 * hh:4 * hh + 4, :], [], wo_keys)
            mixTt = [carve(57344 + k * 1024, 1024, BF16) for k in range(2)]
            chunks = []
            for c in range(8):
                chunks.append((lambda i, c=c: mixTt[i % 2][:, c * 128:(c + 1) * 128],
                               lambda h, c=c: wo_sb[:, c, h * 512:(h + 1) * 512],
                               lambda i, c=c: [("mixTt", i % 2)]))
            wo_phase(l, chunks, wo_loader, wo_keys, tokinfo=(mixtok, mixTt))

        for l in range(n_layers):
            kind = l % 3
            ffn_phase(l, 0, 0)
            if kind == 0:
                fox_phase(l, l // 3)
            elif kind == 1:
                pool_phase(l, l // 3)
            else:
                mla_phase(l, l // 3)
            ffn_phase(l, 1, 2)
        flush_ln()
        sc.barrier()
        fin = []
        for q4 in range(4):
            fin.append(dma("sp", out_d[q4 * 512:(q4 + 1) * 512, :].rearrange("(i p) d -> p i d", p=128),
                           xres[:, 4 * q4:4 * q4 + 4, :], [("x", 4 * q4 + k) for k in range(4)], []))
        final_waits = dict()
        for s_, v_ in fin:
            final_waits[s_] = max(final_waits.get(s_, 0), v_)

        with nc.Block() as block:
            def replay(eng_name, eng):
                for waits, fn, inc in sc.prog[eng_name]:
                    for s_, v_ in waits:
                        eng.wait_ge(sems[s_], v_)
                    ins = fn(eng)
                    if inc is not None:
                        ins.then_inc(sems[inc[0]], inc[1])

            @block.sync
            def _(e):
                replay("sp", e)
                for s_, v_ in final_waits.items():
                    e.wait_ge(sems[s_], v_)

            @block.tensor
            def _(e):
                replay("pe", e)

            @block.scalar
            def _(e):
                replay("act", e)

            @block.vector
            def _(e):
                replay("dve", e)

            @block.gpsimd
            def _(e):
                replay("pool", e)
    return nc


_W_NAMES = ["ln_g", "ln_b", "ffn_w_gate", "ffn_w_up", "ffn_w_down", "mem_w_kv", "w_o", "fox_w_in", "fox_b_f",
            "pool_w_in", "pool_w_grp", "pool_scale", "mla_w_in", "mla_q_norm", "mla_kv_norm", "mla_w_uq", "mla_w_ukv"]


def kernel(**inputs):
    n_cores = 8
    nc = build_program()
    x = np.ascontiguousarray(np.asarray(inputs["x"], dtype=np.float32))
    mem = np.asarray(inputs["mem"], dtype=np.float32)
    pos = np.asarray(inputs["positions"], dtype=np.int32)
    shared = {k: np.ascontiguousarray(np.asarray(inputs[k], dtype=np.float32)) for k in _W_NAMES}
    in_maps = []
    for b in range(n_cores):
        m = dict(shared)
        m["x"] = x[b]
        m["memT"] = np.ascontiguousarray(mem[b].T)
        m["positions"] = np.ascontiguousarray(pos[b].reshape(1, S))
        in_maps.append(m)
    res = run_bass_kernel_spmd(nc, in_maps, core_ids=list(range(n_cores)))
    return np.stack([np.asarray(r["out"], dtype=np.float32) for r in res.results], axis=0)
```
